# Optimizing a Trainium2 kernel written in Bass

```python
import math
import jax, jax.numpy as jnp
from jax import lax
import numpy as np

D_MODEL = 1024
BATCH = 32
SEQ = 2048
DEPTH = 1

CTX_LEN = 256
GRID_W = 64
CHUNK = 2 * GRID_W
ROWS_PER_CHUNK = CHUNK // GRID_W
D_SSM = D_MODEL // 2
SSM_GROUP = 16
N_SSM_GROUPS = D_SSM // SSM_GROUP
SSM_STATE = 64
D_SGU = D_MODEL // 2
N_SGU_GROUPS = 8
SGU_GROUP = D_SGU // N_SGU_GROUPS
D_IN = 2 * D_SSM + 3 * D_SGU + 2 * D_MODEL
DEEPNORM_ALPHA = (2.0 * DEPTH) ** 0.25
DEEPNORM_BETA = (8.0 * DEPTH) ** -0.25
LN_EPS = 1e-6
DT_MIN = 1e-3
DT_MAX = 1e-1

kernel_name = 'hybrid_s5_sgu_parallel_gated_prefix_dit'


def _layer_norm(x):
    x32 = x.astype(jnp.float32)
    mu = jnp.mean(x32, axis=-1, keepdims=True)
    var = jnp.mean(jnp.square(x32 - mu), axis=-1, keepdims=True)
    return ((x32 - mu) * lax.rsqrt(var + LN_EPS)).astype(x.dtype)


def _discretise(lam_re, lam_im, log_dt, b_re, b_im):
    f32 = jnp.float32
    lam_re, lam_im, log_dt = lam_re.astype(f32), lam_im.astype(f32), log_dt.astype(f32)
    b_re, b_im = b_re.astype(f32), b_im.astype(f32)
    dt = jnp.exp(log_dt)[:, None]
    mag = jnp.exp(lam_re * dt)
    ar = mag * jnp.cos(lam_im * dt)
    ai = mag * jnp.sin(lam_im * dt)
    den = lam_re * lam_re + lam_im * lam_im
    nr = ar - 1.0
    fr = (nr * lam_re + ai * lam_im) / den
    fi = (ai * lam_re - nr * lam_im) / den
    bbr = fr[..., None] * b_re - fi[..., None] * b_im
    bbi = fr[..., None] * b_im + fi[..., None] * b_re
    return ar, ai, bbr, bbi


def _complex_affine_combine(e1, e2):
    a1r, a1i, b1r, b1i = e1
    a2r, a2i, b2r, b2i = e2
    return (a1r * a2r - a1i * a2i,
            a1r * a2i + a1i * a2r,
            a2r * b1r - a2i * b1i + b2r,
            a2r * b1i + a2i * b1r + b2i)


def _s5_bidir(u, discs, c_re, c_im, d_skip, h0s, readout):
    bsz, n, _ = u.shape
    ug = u.astype(jnp.float32).reshape(bsz, n, N_SSM_GROUPS, SSM_GROUP)
    y = d_skip.astype(jnp.float32).reshape(N_SSM_GROUPS, SSM_GROUP) * ug if readout else None
    finals = []
    for k, reverse in enumerate((False, True)):
        ar, ai, bbr, bbi = discs[k]
        bu_r = jnp.einsum('blgh,gph->blgp', ug, bbr)
        bu_i = jnp.einsum('blgh,gph->blgp', ug, bbi)
        if h0s is not None:
            h_r, h_i = h0s[k]
            first = n - 1 if reverse else 0
            bu_r = bu_r.at[:, first].add(ar * h_r - ai * h_i)
            bu_i = bu_i.at[:, first].add(ar * h_i + ai * h_r)
        a_r = jnp.broadcast_to(ar, (1, n) + ar.shape)
        a_i = jnp.broadcast_to(ai, (1, n) + ai.shape)
        _, _, s_r, s_i = lax.associative_scan(_complex_affine_combine, (a_r, a_i, bu_r, bu_i),
                                              reverse=reverse, axis=1)
        last = 0 if reverse else n - 1
        finals.append((s_r[:, last], s_i[:, last]))
        if readout:
            y = (y + jnp.einsum('blgp,ghp->blgh', s_r, c_re[k].astype(jnp.float32))
                 - jnp.einsum('blgp,ghp->blgh', s_i, c_im[k].astype(jnp.float32)))
    if readout:
        y = y.reshape(bsz, n, D_SSM)
    return y, finals


def _ssm_glu_gate(y, z, glu_w, glu_b):
    g = jax.nn.gelu(y.astype(z.dtype))
    return g * jax.nn.sigmoid(g @ glu_w + glu_b) * jax.nn.silu(z)


def _sgu_branch(u, v, z, ln_g, ln_b, w_s, b_s, n_chunks):
    bsz, n, _ = u.shape
    u = jax.nn.gelu(u)
    v = _layer_norm(jax.nn.gelu(v)) * ln_g + ln_b
    vg = v.reshape(bsz, n_chunks, CHUNK, N_SGU_GROUPS, SGU_GROUP)
    vm = jnp.einsum('gnm,bkmgc->bkngc', w_s, vg) + b_s.T[None, None, :, :, None]
    return u * vm.reshape(bsz, n, D_SGU) * jax.nn.silu(z)


def _split_proj(p):
    o = np.cumsum([0, D_SSM, D_SSM, D_SGU, D_SGU, D_SGU, D_MODEL, D_MODEL])
    return [p[..., o[j]:o[j + 1]] for j in range(7)]


def _merge_out(ya, yb, ga, gb, w_a, w_b, w_o, b_o):
    m = jax.nn.sigmoid(ga) * (ya @ w_a) + jax.nn.sigmoid(gb) * (yb @ w_b)
    return m @ w_o + b_o


def setup_inputs(seed: int = 0) -> dict:
    key = jax.random.key(seed)
    ks = jax.random.split(key, 32)
    nrm = lambda k, shape, s: jax.random.normal(k, shape, jnp.float32) * s
    G, P, H = N_SSM_GROUPS, SSM_STATE, SSM_GROUP
    lam_im0 = math.pi * jnp.arange(P, dtype=jnp.float32)
    return {
        'x': nrm(ks[0], (BATCH, SEQ, D_MODEL), 1.0),
        'c': nrm(ks[1], (BATCH, D_MODEL), 1.0),
        'ctx': nrm(ks[2], (BATCH, CTX_LEN, D_MODEL), 1.0),
        'c_ctx': nrm(ks[3], (D_MODEL,), 1.0),
        'w_mod': nrm(ks[4], (DEPTH, D_MODEL, 3 * D_MODEL), 0.5 * D_MODEL ** -0.5),
        'b_mod': nrm(ks[5], (DEPTH, 3 * D_MODEL), 0.01),
        'w_in': nrm(ks[6], (DEPTH, D_MODEL, D_IN), D_MODEL ** -0.5),
        'b_in': nrm(ks[7], (DEPTH, D_IN), 0.01),
        'ssm_lam_re': -0.5 + nrm(ks[8], (DEPTH, 2, G, P), 0.01),
        'ssm_lam_im': lam_im0 + nrm(ks[9], (DEPTH, 2, G, P), 0.01),
        'ssm_log_dt': jax.random.uniform(ks[10], (DEPTH, 2, G), jnp.float32,
                                         math.log(DT_MIN), math.log(DT_MAX)),
        'ssm_b_re': nrm(ks[11], (DEPTH, 2, G, P, H), (2.0 * H) ** -0.5),
        'ssm_b_im': nrm(ks[12], (DEPTH, 2, G, P, H), (2.0 * H) ** -0.5),
        'ssm_c_re': nrm(ks[13], (DEPTH, 2, G, H, P), (2.0 * P) ** -0.5),
        'ssm_c_im': nrm(ks[14], (DEPTH, 2, G, H, P), (2.0 * P) ** -0.5),
        'ssm_d': nrm(ks[15], (DEPTH, D_SSM), 1.0),
        'glu_w': nrm(ks[16], (DEPTH, D_SSM, D_SSM), D_SSM ** -0.5),
        'glu_b': nrm(ks[17], (DEPTH, D_SSM), 0.01),
        'sgu_ln_g': 1.0 + nrm(ks[18], (DEPTH, D_SGU), 0.05),
        'sgu_ln_b': nrm(ks[19], (DEPTH, D_SGU), 0.01),
        'sgu_w': nrm(ks[20], (DEPTH, N_SGU_GROUPS, CHUNK, CHUNK), CHUNK ** -0.5),
        'sgu_b': 1.0 + nrm(ks[21], (DEPTH, N_SGU_GROUPS, CHUNK), 0.05),
        'w_branch_a': nrm(ks[22], (DEPTH, D_SSM, D_MODEL), DEEPNORM_BETA * D_SSM ** -0.5),
        'w_branch_b': nrm(ks[23], (DEPTH, D_SGU, D_MODEL), DEEPNORM_BETA * D_SGU ** -0.5),
        'w_out': nrm(ks[24], (DEPTH, D_MODEL, D_MODEL), DEEPNORM_BETA * D_MODEL ** -0.5),
        'b_out': nrm(ks[25], (DEPTH, D_MODEL), 0.01),
        'ln_g': 1.0 + nrm(ks[26], (DEPTH, D_MODEL), 0.05),
        'ln_b': nrm(ks[27], (DEPTH, D_MODEL), 0.01),
    }


def reference(x, c, ctx, c_ctx, w_mod, b_mod, w_in, b_in, ssm_lam_re, ssm_lam_im, ssm_log_dt,
              ssm_b_re, ssm_b_im, ssm_c_re, ssm_c_im, ssm_d, glu_w, glu_b, sgu_ln_g, sgu_ln_b,
              sgu_w, sgu_b, w_branch_a, w_branch_b, w_out, b_out, ln_g, ln_b):
    n_lat = x.shape[1]
    rows = n_lat // GRID_W
    lat_chunks = rows // ROWS_PER_CHUNK
    ctx_chunks = ctx.shape[1] // CHUNK
    for i in range(DEPTH):
        is_last = i == DEPTH - 1
        shift, scale, gate = jnp.split(jax.nn.silu(c) @ w_mod[i] + b_mod[i], 3, axis=-1)
        mod_c = jax.nn.silu(c_ctx) @ w_mod[i] + b_mod[i]
        h = _layer_norm(x) * (1.0 + scale[:, None]) + shift[:, None]
        hc = _layer_norm(ctx) * (1.0 + mod_c[D_MODEL:2 * D_MODEL]) + mod_c[:D_MODEL]

        discs = [_discretise(ssm_lam_re[i, k], ssm_lam_im[i, k], ssm_log_dt[i, k],
                             ssm_b_re[i, k], ssm_b_im[i, k]) for k in range(2)]

        if is_last:
            u_ctx = hc @ w_in[i][:, :D_SSM] + b_in[i][:D_SSM]
            _, ctx_final = _s5_bidir(u_ctx, discs, ssm_c_re[i], ssm_c_im[i], ssm_d[i], None, False)
        else:
            ua_c, za_c, ub_c, vb_c, zb_c, ga_c, gb_c = _split_proj(hc @ w_in[i] + b_in[i])
            y_c, ctx_final = _s5_bidir(ua_c, discs, ssm_c_re[i], ssm_c_im[i], ssm_d[i], None, True)
            ya_c = _ssm_glu_gate(y_c, za_c, glu_w[i], glu_b[i])
            yb_c = _sgu_branch(ub_c, vb_c, zb_c, sgu_ln_g[i], sgu_ln_b[i], sgu_w[i], sgu_b[i], ctx_chunks)
            out_c = _merge_out(ya_c, yb_c, ga_c, gb_c, w_branch_a[i], w_branch_b[i], w_out[i], b_out[i])
            ctx_next = _layer_norm(DEEPNORM_ALPHA * ctx + mod_c[2 * D_MODEL:] * out_c) * ln_g[i] + ln_b[i]

        ua, za, ub, vb, zb, ga, gb = _split_proj(h @ w_in[i] + b_in[i])
        y_a, _ = _s5_bidir(ua, discs, ssm_c_re[i], ssm_c_im[i], ssm_d[i], ctx_final, True)
        ya = _ssm_glu_gate(y_a, za, glu_w[i], glu_b[i])
        yb = _sgu_branch(ub, vb, zb, sgu_ln_g[i], sgu_ln_b[i], sgu_w[i], sgu_b[i], lat_chunks)
        out = _merge_out(ya, yb, ga, gb, w_branch_a[i], w_branch_b[i], w_out[i], b_out[i])
        x = _layer_norm(DEEPNORM_ALPHA * x + gate[:, None] * out) * ln_g[i] + ln_b[i]
        if not is_last:
            ctx = ctx_next
    return x
```

```python
import math
import numpy as np
from contextlib import ExitStack
import concourse.bass as bass
import concourse.mybir as mybir
from concourse.bass_utils import run_bass_kernel_spmd

F32 = mybir.dt.float32
BF16 = mybir.dt.bfloat16
I32 = mybir.dt.int32
ALU = mybir.AluOpType
AF = mybir.ActivationFunctionType

ENGS = ("sp", "act", "dve", "pool", "pe")
NCORES = 8
BPC = 4
L = 2048
D = 1024
CTX = 256
DIN = 4608
G = 32
TWO_PI = 2.0 * math.pi
ALPHA = 2.0 ** 0.25
EPS = 1e-6
TB = 256


class Prog:
    WINDOW = 128
    XLAT = 450.0
    SLAT = 120.0
    TABLE_NS = 1300.0

    def __init__(self, nc):
        self.nc = nc
        self.all = []
        self.last_w = {}
        self.readers = {}
        self.dma_total_streams = set()

    def _deps_for(self, reads, writes):
        deps = []
        for k in reads:
            if k in self.last_w:
                deps.append(self.last_w[k])
        for k in writes:
            if k in self.last_w:
                deps.append(self.last_w[k])
            deps.extend(self.readers.get(k, []))
        return deps

    def _commit(self, oid, reads, writes):
        for k in reads:
            self.readers.setdefault(k, []).append(oid)
        for k in writes:
            self.last_w[k] = oid
            self.readers[k] = []

    def op(self, eng, fn, reads=(), writes=(), dur=300.0, tab=None):
        if eng != "pe":
            writes = list(writes) + [k for k in reads if k.startswith("ps") and k not in writes]
        deps = sorted(set(self._deps_for(reads, writes)))
        oid = len(self.all)
        self.all.append(dict(id=oid, eng=eng, fn=fn, deps=deps, dma=None, dur=dur, barrier=False, tab=tab))
        self._commit(oid, reads, writes)

    def dma(self, eng, out, in_, reads=(), writes=(), stream=None, total=False, dur=3000.0, **kw):
        deps = sorted(set(self._deps_for(reads, writes)))
        if total:
            self.dma_total_streams.add(stream)
            deps = [d for d in deps if not (self.all[d]["dma"] == stream)]
        oid = len(self.all)
        self.all.append(dict(id=oid, eng=eng, fn=lambda e: e.dma_start(out=out, in_=in_, **kw), deps=deps, dma=stream, dur=dur, barrier=False))
        self._commit(oid, reads, writes)

    def wait_all(self, eng, keys):
        deps = sorted(set(self.last_w[k] for k in keys if k in self.last_w))
        self.all.append(dict(id=len(self.all), eng=eng, fn=None, deps=deps, dma=None, dur=10.0, barrier=False))

    def barrier(self):
        self.all.append(dict(id=len(self.all), eng=None, fn=None, deps=[], dma=None, dur=0.0, barrier=True))
        self.last_w = {}
        self.readers = {}

    def _schedule_segment(self, seg, t_base):
        fin = {}
        feng = {}
        cur_tab = [None]
        pend = {e: [o for o in seg if o["eng"] == e] for e in ENGS}
        head = {e: 0 for e in ENGS}
        done = set()
        free = {e: t_base for e in ENGS}
        seg_ids = set(o["id"] for o in seg)
        order = []
        remaining = len(seg)
        W = self.WINDOW
        while remaining:
            best = None
            for e in ENGS:
                lst = pend[e]
                h = head[e]
                cnt = 0
                j = h
                while j < len(lst) and cnt < W:
                    o = lst[j]
                    j += 1
                    if o["id"] in done:
                        continue
                    cnt += 1
                    ok = True
                    st = free[e]
                    for d in o["deps"]:
                        if d in seg_ids:
                            if d not in fin:
                                ok = False
                                break
                            fd = fin[d] + (self.SLAT if feng[d] == e else self.XLAT)
                            if e == "pe" and feng[d] == "pe":
                                fd = fin[d] - 200.0
                            if fd > st:
                                st = fd
                    if not ok:
                        continue
                    pen = 0.0
                    if e == "act" and o.get("tab") is not None and o["tab"] != cur_tab[0]:
                        pen = self.TABLE_NS
                    key = (st + pen, o["id"], st)
                    if best is None or key < best[0]:
                        best = (key, e, o)
                    if st + pen <= free[e]:
                        break
            assert best is not None, "scheduler stuck"
            (stp, _, _st), e, o = best
            st = stp
            issue = 60.0 if o["dma"] is not None else o["dur"]
            if e == "act" and o.get("tab") is not None:
                cur_tab[0] = o["tab"]
            free[e] = st + issue
            fin[o["id"]] = st + o["dur"]
            feng[o["id"]] = e
            done.add(o["id"])
            order.append(o)
            remaining -= 1
            lst = pend[e]
            while head[e] < len(lst) and lst[head[e]]["id"] in done:
                head[e] += 1
        t_end = max([t_base] + list(fin.values()))
        return order, t_end

    def emit(self, stack):
        nc = self.nc
        segs = [[]]
        for o in self.all:
            if o["barrier"]:
                segs.append([])
            else:
                segs[-1].append(o)
        seq = []
        t = 0.0
        for si, seg in enumerate(segs):
            if si > 0:
                seq.append(None)
            order, t = self._schedule_segment(seg, t)
            seq.extend(order)
        self.est_ns = t
        streams = {e: [] for e in ENGS}
        pos = {}
        dma_rank = {}
        dma_cnt = {}
        last_compute = {e: None for e in ENGS}
        for o in seq:
            if o is None:
                deps = [v for v in last_compute.values() if v is not None]
                dmas = dict(dma_cnt)
                for e in ENGS:
                    streams[e].append(dict(fn=None, deps=list(deps), dma=None, dma_totals=dmas, id=-1))
                continue
            e = o["eng"]
            pos[o["id"]] = (e, len(streams[e]))
            streams[e].append(o)
            if o["dma"] is not None:
                dma_cnt[o["dma"]] = dma_cnt.get(o["dma"], 0) + 1
                dma_rank[o["id"]] = dma_cnt[o["dma"]]
            elif o["fn"] is not None:
                last_compute[e] = o["id"]
        self.dma_cnt = dma_cnt
        by_id = {o["id"]: o for o in self.all}
        target = {e: [False] * len(streams[e]) for e in ENGS}
        for e in ENGS:
            for o in streams[e]:
                for d in o["deps"]:
                    od = by_id[d]
                    if od["dma"] is None and od["fn"] is not None and not (od["eng"] == "pe" and e == "pe"):
                        pe_, pi_ = pos[d]
                        target[pe_][pi_] = True
        cum = {}
        for e in ENGS:
            c = 0
            cum[e] = []
            for i in range(len(streams[e])):
                if target[e][i]:
                    c += 1
                cum[e].append(c)
        sems = {e: stack.enter_context(nc.semaphore("s_" + e)) for e in ENGS}
        dsems = {s: stack.enter_context(nc.semaphore("d_" + s)) for s in dma_cnt}
        block = stack.enter_context(nc.Block())
        self.ops = streams

        def run(ename, e):
            seen = {}
            for i, o in enumerate(streams[ename]):
                waits = {}
                for d in o["deps"]:
                    od = by_id[d]
                    if od["dma"] is not None:
                        key = ("d", od["dma"])
                        n = dma_cnt[od["dma"]] if od["dma"] in self.dma_total_streams else dma_rank[d]
                        val = 16 * n
                    elif od["fn"] is None:
                        continue
                    else:
                        if od["eng"] == "pe" and ename == "pe":
                            continue
                        pe_, pi_ = pos[d]
                        key = ("e", pe_)
                        val = cum[pe_][pi_]
                    if val > waits.get(key, 0):
                        waits[key] = val
                if o.get("dma_totals"):
                    for s_, n_ in o["dma_totals"].items():
                        key = ("d", s_)
                        if 16 * n_ > waits.get(key, 0):
                            waits[key] = 16 * n_
                for key, val in waits.items():
                    if seen.get(key, 0) >= val:
                        continue
                    seen[key] = val
                    e.wait_ge(sems[key[1]] if key[0] == "e" else dsems[key[1]], val)
                if o["fn"] is None:
                    continue
                ins = o["fn"](e)
                if o["dma"] is not None:
                    ins.then_inc(dsems[o["dma"]], 16)
                elif target[ename][i]:
                    ins.then_inc(sems[ename], 1)

        @block.sync
        def _(e):
            run("sp", e)

        @block.scalar
        def _(e):
            run("act", e)

        @block.vector
        def _(e):
            run("dve", e)

        @block.gpsimd
        def _(e):
            run("pool", e)

        @block.tensor
        def _(e):
            run("pe", e)


class Arena:
    def __init__(self, t):
        self.f = t[:, :]
        self.b = t[:, :].bitcast(BF16)
        self.i = t[:, :].bitcast(I32)
        self.top = 0
        self.cap = t.shape[1]

    def f32(self, n):
        off = self.top
        self.top += n
        assert self.top <= self.cap, ("arena overflow", self.top, self.cap)
        return self.f[:, off:off + n]

    def i32(self, n):
        off = self.top
        self.top += n
        assert self.top <= self.cap, ("arena overflow", self.top, self.cap)
        return self.i[:, off:off + n]

    def bf(self, n):
        n2 = (n + 1) // 2
        off = self.top
        self.top += n2
        assert self.top <= self.cap, ("arena overflow", self.top, self.cap)
        return self.b[:, 2 * off:2 * off + n]


INPUT_SPECS = [
    ("x", [BPC, L, D]), ("c_all", [BPC + 1, D]), ("ctx", [BPC * CTX, D]),
    ("w_mod", [D, 3 * D]), ("b_mod", [3 * D]), ("w_in", [D, DIN]), ("b_in", [DIN]),
    ("lam_re", [2, G, 64]), ("lam_im", [2, G, 64]), ("log_dt", [2, G]),
    ("b_re", [2, G, 64, 16]), ("b_im", [2, G, 64, 16]), ("c_re", [2, G, 16, 64]), ("c_im", [2, G, 16, 64]),
    ("ssm_d", [512]), ("glu_w", [512, 512]), ("glu_b", [512]), ("sgu_ln_g", [512]), ("sgu_ln_b", [512]),
    ("sgu_w", [8, 128, 128]), ("sgu_b", [8, 128]), ("w_a", [512, D]), ("w_b", [512, D]),
    ("w_out", [D, D]), ("b_out", [D]), ("ln_g", [D]), ("ln_b", [D]),
    ("k_ident", [128, 128]), ("k_ein", [128, 24]), ("k_eout", [128, 25]),
    ("k_mge", [128, 128]), ("k_mle", [128, 128]), ("k_sel", [5, 5 * 128]),
]


def host_consts():
    ident = np.eye(128, dtype=np.float32)
    ein = np.zeros((128, 24), np.float32)
    eout = np.zeros((128, 25), np.float32)
    for i in range(16):
        ein[0:64, i] = 15 - i
        ein[64:128, i] = i
    for i in range(16, 24):
        ein[0:64, i] = -(i - 16)
    for i in range(17):
        eout[0:64, i] = i
        eout[64:128, i] = 17 - i
    for i in range(17, 25):
        eout[64:128, i] = -(i - 17)
    jj = np.arange(128) // 16
    mge = (jj[None, :] >= jj[:, None]).astype(np.float32)
    mle = (jj[None, :] <= jj[:, None]).astype(np.float32)
    sel = np.zeros((5, 5, 128), np.float32)
    for b in range(5):
        sel[b, b, :] = 1.0
    return dict(k_ident=ident, k_ein=ein, k_eout=eout, k_mge=mge, k_mle=mle, k_sel=sel.reshape(5, 640))


def build_program(stop=None):
    nc = bass.Bass("TRN2", target_bir_lowering=False)
    dr = {}
    for name, shape in INPUT_SPECS:
        dr[name] = nc.dram_tensor(name, shape, F32, kind="ExternalInput").ap()
    out_d = nc.dram_tensor("out", [BPC, L, D], F32, kind="ExternalOutput").ap()
    yscr = nc.dram_tensor("yscr", [BPC, L, 512], BF16, kind="Internal").ap()

    st = ExitStack()
    with st:
        arena_t = st.enter_context(nc.sbuf_tensor("arena", [128, 52736], F32))
        psf = [st.enter_context(nc.psum_tensor("psf%d" % i, [128, 512], F32)) for i in range(6)]
        psb = [st.enter_context(nc.psum_tensor("psb%d" % i, [128, 1024], BF16)) for i in range(2)]
        A = Arena(arena_t)
        p = Prog(nc)

        dbg_list = []

        def finish():
            p.barrier()
            col = 0
            row = 0
            for nm_, ap_ in dbg_list:
                w_ = ap_.shape[1]
                if col + w_ > 1024:
                    col = 0
                    row += 128
                p.dma("sp", out_d[0, row:row + ap_.shape[0], col:col + w_], ap_, stream="dbg", total=True)
                print("DBG", nm_, row, col, ap_.shape)
                col += w_
            p.barrier()
            p.emit(st)
            print("ops:", {e: len(p.ops[e]) for e in ENGS}, "arena top", A.top, "stop", stop, "est_us", p.est_ns / 1e3)
            return nc

        def nfree(ap):
            n = 1
            for d_ in ap.shape[1:]:
                n *= d_
            return n

        def edur(eng, ap):
            n = nfree(ap)
            if eng == "act":
                return 330.0 + 0.75 * n
            if eng == "dve":
                return 150.0 + 1.05 * n
            if eng == "pool":
                return 200.0 + 2.3 * n
            return 300.0

        TABS = {AF.Gelu_apprx_tanh: "gelu", AF.Silu: "silu", AF.Sigmoid: "sigmoid", AF.Sqrt: "sqrt", AF.Exp: "exp", AF.Sin: "sin"}

        def ACT(out, in_, func, reads, writes, scale=1.0, bias=0.0):
            p.op("act", lambda e: e.activation(out=out, in_=in_, func=func, scale=scale, bias=bias), reads, writes, dur=edur("act", out),
                 tab=TABS.get(func))

        def TT(eng, out, a, b, op, reads, writes):
            p.op(eng, lambda e: e.tensor_tensor(out=out, in0=a, in1=b, op=op), reads, writes, dur=edur(eng, out))

        def TS(eng, out, a, s1, s2, op0, op1, reads, writes):
            if s2 is None:
                p.op(eng, lambda e: e.tensor_scalar(out=out, in0=a, scalar1=s1, scalar2=None, op0=op0), reads, writes, dur=edur(eng, out))
            else:
                p.op(eng, lambda e: e.tensor_scalar(out=out, in0=a, scalar1=s1, scalar2=s2, op0=op0, op1=op1), reads, writes, dur=edur(eng, out))

        def STT(out, a, s, b, op0, op1, reads, writes):
            p.op("dve", lambda e: e.scalar_tensor_tensor(out=out, in0=a, scalar=s, in1=b, op0=op0, op1=op1), reads, writes, dur=edur("dve", out))

        def CP(eng, out, in_, reads, writes):
            if eng == "act":
                ACT(out, in_, AF.Copy, reads, writes)
            else:
                p.op(eng, lambda e: e.tensor_copy(out=out, in_=in_), reads, writes, dur=edur(eng, out))

        def MM(out, lhsT, rhs, start, stop, reads, writes):
            p.op("pe", lambda e: e.matmul(out, lhsT=lhsT, rhs=rhs, start=start, stop=stop), reads, writes,
                 dur=(max(nfree(rhs), 64) / 2.2 + 10.0) * (4.0 if rhs.dtype == F32 else 1.0))

        def TR(out, in_, ident, reads, writes):
            p.op("pe", lambda e: e.transpose(out, in_, ident), reads, writes, dur=70.0 * (4.0 if in_.dtype == F32 else 1.0))

        def LD(eng, out, in_, writes, stream, reads=(), total=True, **kw):
            nb = nfree(out) * out.shape[0] * (2 if out.dtype == BF16 else 4)
            p.dma(eng, out, in_, reads=reads, writes=writes, stream=stream, total=total, dur=2500.0 + nb / 100.0, **kw)

        def bc(ap, shape, axis):
            return ap.unsqueeze(axis).to_broadcast(shape)

        ident_bf = A.bf(128)
        ident_f = A.f32(128)
        ST = A.f32(16 * 5)
        ST3 = ST.rearrange("p (i b) -> p i b", b=5)
        grow = A.f32(1024)
        sel = A.f32(640)
        mhalf = A.f32(2)
        p.op("dve", lambda e: e.memset(mhalf, -0.5), [], ["mhalf"])
        small_top = A.top
        W_in = A.bf(G * 2 * 2 * 128)
        W_in5 = W_in.rearrange("p (g j r m) -> p g j r m", g=G, j=2, r=2)
        W_oR = A.bf(G * 16 * 16)
        W_oI = A.bf(G * 16 * 16)
        W_oR4 = W_oR.rearrange("p (g t h) -> p g t h", g=G, t=16)
        W_oI4 = W_oI.rearrange("p (g t h) -> p g t h", g=G, t=16)
        Tz = A.bf(G * 3 * 128)
        Tz4 = Tz.rearrange("p (g d m) -> p g d m", g=G, d=3)
        AA1 = A.f32(64)
        AA2 = A.f32(64)
        AA1_3 = AA1.rearrange("p (r g) -> p r g", r=2)
        AA2_3 = AA2.rearrange("p (r g) -> p r g", r=2)
        APW1 = [None] + [A.f32(64).rearrange("p (r g) -> p r g", r=2) for _ in range(8)]
        APW2 = [None] + [A.f32(64).rearrange("p (r g) -> p r g", r=2) for _ in range(8)]
        H0 = A.f32(2 * G * BPC)
        H04 = H0.rearrange("p (r g b) -> p r g b", r=2, g=G)
        persist_top = A.top

        LD("pool", ident_bf, dr["k_ident"], ["ident_bf"], "c0")
        LD("sp", ident_f, dr["k_ident"], ["ident_f"], "c1")
        LD("sp", sel[0:5, :], dr["k_sel"], ["sel"], "c1")

        modrow = A.f32(3072)
        scT = A.f32(40)
        scT3 = scT.rearrange("p (k b) -> p k b", k=8)
        wm = [A.f32(3072), A.f32(3072)]
        bmod_rep = A.f32(3072)
        for b_ in range(BPC + 1):
            LD("sp", scT3[:, :, b_], dr["c_all"][b_].rearrange("(k p) -> p k", p=128), ["scT"], "c1", allow_slow_non_contiguous=True)
        LD("sp", bmod_rep[0:5, :], dr["b_mod"].partition_broadcast(5), ["bmod"], "c1")
        ACT(scT, scT, AF.Silu, ["scT"], ["scT"])
        for kt in range(8):
            LD("sp" if kt % 2 == 0 else "act", wm[kt % 2], dr["w_mod"][kt * 128:(kt + 1) * 128, :], ["wm%d" % (kt % 2)],
               "wm%d" % (kt % 2), total=False)
            for cb in range(6):
                MM(psf[cb][0:5, :], scT3[:, kt, :], wm[kt % 2][:, cb * 512:(cb + 1) * 512], kt == 0, kt == 7,
                   ["scT", "wm%d" % (kt % 2)], ["psf%d" % cb])
        for cb in range(6):
            TT("dve", modrow[0:5, cb * 512:(cb + 1) * 512], psf[cb][0:5, :], bmod_rep[0:5, cb * 512:(cb + 1) * 512], ALU.add,
               ["psf%d" % cb, "bmod"], ["modrow"])
        for i in range(16):
            TR(psf[0][:, i * 5:(i + 1) * 5], modrow[0:5, i * 128:(i + 1) * 128], ident_f[0:5, 0:5], ["modrow", "ident_f"], ["psf0"])
        CP("dve", ST, psf[0][:, 0:80], ["psf0"], ["ST"])
        TS("dve", ST[:, 40:80], ST[:, 40:80], 1.0, None, ALU.add, None, ["ST"], ["ST"])
        CP("dve", grow[0:5, :], modrow[0:5, 2048:3072], ["modrow"], ["grow"])
        p.barrier()
        if stop == "0a":
            return finish()
        A.top = persist_top

        import os as _os2
        for _i in range(int(_os2.environ.get("PADDVE", "0"))):
            p.op("dve", lambda e: e.memset(ST[:, 0:1], 0.0) if False else e.tensor_copy(out=modrow[0:5, 0:8], in_=modrow[0:5, 8:16]), [], [])
        LR = A.f32(G); LI = A.f32(G); DTl = A.f32(G); RHO = A.f32(G); TH = A.f32(G)
        Lst = A.f32(256)
        Lst4 = Lst.rearrange("g (a d q) -> g a d q", a=2, d=2)
        for a_, nm in enumerate(("lam_re", "lam_im")):
            for d_ in range(2):
                LD("sp", Lst4[0:32, a_, d_, :], dr[nm][d_], ["Lst"], "c2")
        for d_ in range(2):
            LD("sp", DTl[d_ * 64:(d_ + 1) * 64, :], dr["log_dt"][d_].partition_broadcast(64), ["DTl"], "c2")
        Ein = A.f32(24); Eout = A.f32(25)
        LD("sp", Ein, dr["k_ein"], ["Ein"], "c2")
        LD("sp", Eout, dr["k_eout"], ["Eout"], "c2")
        Mge = A.f32(128); Mle = A.f32(128)
        LD("sp", Mge, dr["k_mge"], ["Mge"], "c2")
        LD("sp", Mle, dr["k_mle"], ["Mle"], "c2")
        TR(psf[0][:, 0:32], Lst[0:32, 0:128], ident_f[0:32, 0:32], ["Lst", "ident_f"], ["psf0"])
        TR(psf[0][:, 32:64], Lst[0:32, 128:256], ident_f[0:32, 0:32], ["Lst", "ident_f"], ["psf0"])
        CP("dve", LR, psf[0][:, 0:32], ["psf0"], ["LR"])
        CP("dve", LI, psf[0][:, 32:64], ["psf0"], ["LI"])
        ACT(DTl, DTl, AF.Exp, ["DTl"], ["DTl"])
        TT("dve", RHO, LR, DTl, ALU.mult, ["LR", "DTl"], ["RHO"])
        TT("dve", TH, LI, DTl, ALU.mult, ["LI", "DTl"], ["TH"])
        Dst = A.f32(128)
        Dst3 = Dst.rearrange("g (j h) -> g j h", j=8)
        for j in range(8):
            LD("sp", Dst3[0:32, j, :], dr["ssm_d"].rearrange("(g h) -> g h", h=16), ["Dst"], "c2")
        Dv = A.f32(G)
        TR(psf[1][:, 0:32], Dst[0:32, :], ident_f[0:32, 0:32], ["Dst", "ident_f"], ["psf1"])
        CP("dve", Dv, psf[1][:, 0:32], ["psf1"], ["Dv"])
        if stop == "0b1":
            return finish()

        PT = dict(PHI=A.f32(G * 25), K=A.i32(G * 25), Kf=A.f32(G * 25), R=A.f32(G * 25), S=A.f32(G * 25), C=A.f32(G * 25), M=A.f32(G * 25))

        def power_table(E, n, nm):
            sz = G * n
            PHI = PT["PHI"][:, 0:sz]; K = PT["K"][:, 0:sz]; Kf = PT["Kf"][:, 0:sz]; R = PT["R"][:, 0:sz]
            SINv = PT["S"][:, 0:sz]; COSv = PT["C"][:, 0:sz]; MAG = PT["M"][:, 0:sz]
            AR = A.f32(sz); AI = A.f32(sz)
            nm_e = "Ein" if nm == "i" else "Eout"
            nm = ""
            v3 = lambda t: t.rearrange("p (g n) -> p g n", g=G)
            TT("dve", v3(PHI), bc(TH, [128, G, n], 2), bc(E, [128, G, n], 1), ALU.mult, ["TH", nm_e], ["PHI" + nm])
            for which, dst in ((0, SINv), (1, COSv)):
                src = PHI
                if which == 1:
                    TS("dve", R, PHI, math.pi / 2, None, ALU.add, None, ["PHI" + nm], ["R" + nm])
                    src = R
                TS("dve", K, src, 1.0 / TWO_PI, None, ALU.mult, None, ["PHI" + nm, "R" + nm], ["K" + nm])
                CP("dve", Kf, K, ["K" + nm], ["Kf" + nm])
                STT(R, Kf, -TWO_PI, src, ALU.mult, ALU.add, ["Kf" + nm, "PHI" + nm, "R" + nm], ["R" + nm])
                TS("dve", R, R, 3.14159, -3.14159, ALU.min, ALU.max, ["R" + nm], ["R" + nm])
                ACT(dst, R, AF.Sin, ["R" + nm], ["sc%d" % which + nm])
            TT("dve", v3(MAG), bc(RHO, [128, G, n], 2), bc(E, [128, G, n], 1), ALU.mult, ["RHO", nm_e], ["MAG" + nm])
            ACT(MAG, MAG, AF.Exp, ["MAG" + nm], ["MAG" + nm])
            sfx = "i" if nm_e == "Ein" else "o"
            TT("dve", AR, MAG, COSv, ALU.mult, ["MAG" + nm, "sc1" + nm], ["AR" + sfx])
            TT("dve", AI, MAG, SINv, ALU.mult, ["MAG" + nm, "sc0" + nm], ["AI" + sfx])
            return v3(AR), v3(AI)

        ARi, AIi = power_table(Ein, 24, "i")
        ARo, AIo = power_table(Eout, 25, "o")

        if stop == "0b2":
            return finish()
        A1r = A.f32(G); A1i = A.f32(G)
        for (lo, hi, i1, i16) in ((0, 64, 1, 16), (64, 128, 16, 1)):
            CP("dve", A1r[lo:hi, :], ARo[lo:hi, :, i1], ["ARo"], ["A1r"])
            CP("dve", A1i[lo:hi, :], AIo[lo:hi, :, i1], ["AIo"], ["A1i"])
            CP("dve", AA1_3[lo:hi, 0, :], ARo[lo:hi, :, i16], ["ARo"], ["AA1"])
            CP("dve", AA1_3[lo:hi, 1, :], ARo[lo:hi, :, i16], ["ARo"], ["AA1"])
            CP("dve", AA2_3[lo:hi, 1, :], AIo[lo:hi, :, i16], ["AIo"], ["AA2"])
            TS("dve", AA2_3[lo:hi, 0, :], AIo[lo:hi, :, i16], -1.0, None, ALU.mult, None, ["AIo"], ["AA2"])
        pw_t = [A.f32(G), A.f32(G)]
        CP("dve", APW1[1], AA1_3, ["AA1"], ["APW"])
        CP("dve", APW2[1], AA2_3, ["AA2"], ["APW"])
        for m in range(2, 9):
            pr_, pi_ = APW1[m - 1][:, 0, :], APW2[m - 1][:, 1, :]
            ar_, ai_ = AA1_3[:, 0, :], AA2_3[:, 1, :]
            TT("dve", pw_t[0], pr_, ar_, ALU.mult, ["APW", "AA1"], ["pw0"])
            TT("dve", pw_t[1], pi_, ai_, ALU.mult, ["APW", "AA2"], ["pw1"])
            TT("dve", APW1[m][:, 0, :], pw_t[0], pw_t[1], ALU.subtract, ["pw0", "pw1"], ["APW"])
            CP("dve", APW1[m][:, 1, :], APW1[m][:, 0, :], ["APW"], ["APW"])
            TT("dve", pw_t[0], pr_, ai_, ALU.mult, ["APW", "AA2"], ["pw0"])
            TT("dve", pw_t[1], pi_, ar_, ALU.mult, ["APW", "AA1"], ["pw1"])
            TT("dve", APW2[m][:, 1, :], pw_t[0], pw_t[1], ALU.add, ["pw0", "pw1"], ["APW"])
            TS("dve", APW2[m][:, 0, :], APW2[m][:, 1, :], -1.0, None, ALU.mult, None, ["APW"], ["APW"])
        den = A.f32(G); nr = A.f32(G); fr = A.f32(G); fi = A.f32(G); t1 = A.f32(G); t2 = A.f32(G)
        TT("dve", den, LR, LR, ALU.mult, ["LR"], ["den"])
        TT("dve", t1, LI, LI, ALU.mult, ["LI"], ["t1"])
        TT("dve", den, den, t1, ALU.add, ["den", "t1"], ["den"])
        p.op("dve", lambda e: e.reciprocal(out=den, in_=den), ["den"], ["den"])
        TS("dve", nr, A1r, -1.0, None, ALU.add, None, ["A1r"], ["nr"])
        TT("dve", t1, nr, LR, ALU.mult, ["nr", "LR"], ["t1"])
        TT("dve", t2, A1i, LI, ALU.mult, ["A1i", "LI"], ["t2"])
        TT("dve", t1, t1, t2, ALU.add, ["t1", "t2"], ["t1"])
        TT("dve", fr, t1, den, ALU.mult, ["t1", "den"], ["fr"])
        TT("dve", t1, A1i, LR, ALU.mult, ["A1i", "LR"], ["t1"])
        TT("dve", t2, nr, LI, ALU.mult, ["nr", "LI"], ["t2"])
        TT("dve", t1, t1, t2, ALU.subtract, ["t1", "t2"], ["t1"])
        TT("dve", fi, t1, den, ALU.mult, ["t1", "den"], ["fi"])
        BR = A.f32(G * 16); BI = A.f32(G * 16); Bbr = A.f32(G * 16); Bbi = A.f32(G * 16); tB = A.f32(G * 16)
        g3 = lambda t: t.rearrange("p (g h) -> p g h", g=G)
        for nm, dst in (("b_re", BR), ("b_im", BI)):
            for d_ in range(2):
                for q in range(4):
                    LD("sp" if q % 2 == 0 else "act", g3(dst)[d_ * 64:(d_ + 1) * 64, q * 8:(q + 1) * 8, :],
                       dr[nm][d_, q * 8:(q + 1) * 8].rearrange("g p h -> p g h"), [nm], "c3")
        Fr3 = bc(fr, [128, G, 16], 2); Fi3 = bc(fi, [128, G, 16], 2)
        TT("dve", g3(Bbr), g3(BR), Fr3, ALU.mult, ["b_re", "fr"], ["Bbr"])
        TT("dve", g3(tB), g3(BI), Fi3, ALU.mult, ["b_im", "fi"], ["tB"])
        TT("dve", Bbr, Bbr, tB, ALU.subtract, ["Bbr", "tB"], ["Bbr"])
        TT("dve", g3(Bbi), g3(BI), Fr3, ALU.mult, ["b_im", "fr"], ["Bbi"])
        TT("dve", g3(tB), g3(BR), Fi3, ALU.mult, ["b_re", "fi"], ["tB"])
        TT("dve", Bbi, Bbi, tB, ALU.add, ["Bbi", "tB"], ["Bbi"])
        Cst = A.f32(4 * 2 * 64)
        Cst4 = Cst.rearrange("p (t d q) -> p t d q", t=4, d=2)
        CR = A.f32(G * 16); CI = A.f32(G * 16); nCR = A.f32(G * 16); nCI = A.f32(G * 16)
        Cst_b = A.f32(4 * 2 * 64)
        Cst4_b = Cst_b.rearrange("p (t d q) -> p t d q", t=4, d=2)
        for nm, dst, c4, ck in (("c_re", CR, Cst4, "CstA"), ("c_im", CI, Cst4_b, "CstB")):
            for d_ in range(2):
                LD("sp", c4[:, :, d_, :], dr[nm][d_].rearrange("(t g) h q -> (g h) t q", t=4), [ck], "cst" + nm, total=True)
            for t_ in range(4):
                TR(psf[2][:, t_ * 128:(t_ + 1) * 128], c4[:, t_, :, :].rearrange("p d q -> p (d q)"), ident_f, [ck, "ident_f"], ["psf2"])
            CP("dve", dst, psf[2][:, :], ["psf2"], [nm])
        TS("dve", nCR, CR, -1.0, None, ALU.mult, None, ["c_re"], ["nCR"])
        TS("dve", nCI, CI, -1.0, None, ALU.mult, None, ["c_im"], ["nCI"])

        if stop == "0b3":
            return finish()
        GQ = 4
        GB_R = A.bf(GQ * 24 * 16); GB_I = A.bf(GQ * 24 * 16)
        GC_R = A.bf(GQ * 25 * 16); GC_I = A.bf(GQ * 25 * 16)
        GB_R4 = GB_R.rearrange("p (g n h) -> p g n h", g=GQ, n=24); GB_I4 = GB_I.rearrange("p (g n h) -> p g n h", g=GQ, n=24)
        GC_R4 = GC_R.rearrange("p (g n h) -> p g n h", g=GQ, n=25); GC_I4 = GC_I.rearrange("p (g n h) -> p g n h", g=GQ, n=25)
        tmpA = [A.f32(GQ * 25 * 16), A.f32(GQ * 25 * 16)]
        tmpP = [A.f32(GQ * 25 * 16), A.f32(GQ * 25 * 16)]
        tz1 = A.f32(128); tz2 = A.f32(128)
        fl = lambda ap: ap.rearrange("p j h -> p (j h)")
        for q in range(G // GQ):
            gs = slice(q * GQ, (q + 1) * GQ)
            shp = [128, GQ, 24, 16]
            ta = [t[:, 0:GQ * 24 * 16].rearrange("p (g n h) -> p g n h", g=GQ, n=24) for t in tmpA]
            a_r = bc(ARi[:, gs, :], shp, 3); a_i = bc(AIi[:, gs, :], shp, 3)
            b_r = bc(g3(Bbr)[:, gs, :], shp, 2); b_i = bc(g3(Bbi)[:, gs, :], shp, 2)
            TT("dve", ta[0], a_r, b_r, ALU.mult, ["ARi", "Bbr"], ["tA0"])
            TT("dve", ta[1], a_i, b_i, ALU.mult, ["AIi", "Bbi"], ["tA1"])
            TT("dve", GB_R4, ta[0], ta[1], ALU.subtract, ["tA0", "tA1"], ["GB_R"])
            TT("dve", ta[0], a_i, b_r, ALU.mult, ["AIi", "Bbr"], ["tA0"])
            TT("dve", ta[1], a_r, b_i, ALU.mult, ["ARi", "Bbi"], ["tA1"])
            TT("dve", GB_I4, ta[0], ta[1], ALU.add, ["tA0", "tA1"], ["GB_I"])
            p.op("dve", lambda e: e.memset(GB_R4[64:128, :, 16:24, :], 0.0), [], ["GB_R"])
            p.op("dve", lambda e: e.memset(GB_I4[64:128, :, 16:24, :], 0.0), [], ["GB_I"])
            if stop == "g1":
                return finish()
            shp = [128, GQ, 25, 16]
            tp = [t.rearrange("p (g n h) -> p g n h", g=GQ, n=25) for t in tmpP]
            o_r = bc(ARo[:, gs, :], shp, 3); o_i = bc(AIo[:, gs, :], shp, 3)
            c_r = bc(g3(CR)[:, gs, :], shp, 2); c_i = bc(g3(CI)[:, gs, :], shp, 2)
            nc_r = bc(g3(nCR)[:, gs, :], shp, 2); nc_i = bc(g3(nCI)[:, gs, :], shp, 2)
            TT("pool", tp[0], o_r, c_r, ALU.mult, ["ARo", "c_re"], ["tP0"])
            TT("pool", tp[1], o_i, c_i, ALU.mult, ["AIo", "c_im"], ["tP1"])
            TT("pool", GC_R4, tp[0], tp[1], ALU.subtract, ["tP0", "tP1"], ["GC_R"])
            TT("pool", tp[0], o_i, nc_r, ALU.mult, ["AIo", "nCR"], ["tP0"])
            TT("pool", tp[1], o_r, nc_i, ALU.mult, ["ARo", "nCI"], ["tP1"])
            TT("pool", GC_I4, tp[0], tp[1], ALU.add, ["tP0", "tP1"], ["GC_I"])
            p.op("pool", lambda e: e.memset(GC_R4[0:64, :, 17:25, :], 0.0), [], ["GC_R"])
            p.op("pool", lambda e: e.memset(GC_I4[0:64, :, 17:25, :], 0.0), [], ["GC_I"])
            if stop == "g2":
                return finish()
            CP("pool", W_oR4[:, gs], GC_R4[:, :, 1:17, :], ["GC_R"], ["W_oR"])
            CP("pool", W_oI4[:, gs], GC_I4[:, :, 1:17, :], ["GC_I"], ["W_oI"])
            if stop == "g3":
                return finish()
            for g2 in range(0, GQ, 2):
                pb = psb[(g2 // 2) % 2]
                pk = "psb%d" % ((g2 // 2) % 2)
                k = 0
                for gl in (g2, g2 + 1):
                    for J in range(2):
                        for ri, GB4 in enumerate((GB_R4, GB_I4)):
                            TR(pb[:, k * 128:(k + 1) * 128], fl(GB4[:, gl, 8 * J:8 * J + 8, :]), ident_bf,
                               ["GB_R", "GB_I", "ident_bf"], [pk])
                            k += 1
                gg_ = q * GQ + g2
                CP("act", W_in[:, gg_ * 512:(gg_ + 2) * 512], pb[:, :], [pk], ["W_in"])
            if stop == "g4":
                dbg_list.extend([("ARi", ARi.rearrange("p g n -> p (g n)")), ("AIi", AIi.rearrange("p g n -> p (g n)")),
                                 ("ARo", ARo.rearrange("p g n -> p (g n)")), ("AIo", AIo.rearrange("p g n -> p (g n)")),
                                 ("Bbr", Bbr), ("Bbi", Bbi), ("CR", CR), ("CI", CI), ("Dv", Dv), ("fr", fr), ("fi", fi),
                                 ("LR", LR), ("LI", LI), ("DTl", DTl)])
                return finish()
            for gl in range(GQ):
                g = q * GQ + gl
                import os as _os
                _alt = 0 if _os.environ.get("NOALT") else (g % 2)
                psF = psf[2 + 2 * _alt]; pkF = "psf%d" % (2 + 2 * _alt)
                psB = psf[3 + 2 * _alt]; pkB = "psf%d" % (3 + 2 * _alt)
                specs = [
                    (psF, pkF, 0, (16, 24), (8, 16)),
                    (psF, pkF, 1, (16, 24), (0, 8)),
                    (psB, pkB, 0, (8, 16), (17, 25)),
                    (psB, pkB, 1, (0, 8), (17, 25)),
                ]
                for (ps_, pk, k, bi, ci) in specs:
                    MM(ps_[:, k * 128:(k + 1) * 128], fl(GB_R4[:, gl, bi[0]:bi[1], :]), fl(GC_R4[:, gl, ci[0]:ci[1], :]), True, False,
                       ["GB_R", "GC_R"], [pk])
                    MM(ps_[:, k * 128:(k + 1) * 128], fl(GB_I4[:, gl, bi[0]:bi[1], :]), fl(GC_I4[:, gl, ci[0]:ci[1], :]), False, True,
                       ["GB_I", "GC_I"], [pk])
                if stop == "g7" and gl == 1:
                    return finish()
                _ce = "dve"
                CP(_ce, Tz4[:, g, 0, :], psF[:, 0:128], [pkF], ["Tz"])
                CP(_ce, Tz4[:, g, 1, :], psB[:, 0:128], [pkB], ["Tz"])
                if stop == "g8" and gl == 1:
                    CP("act", tmpP[0][:, 0:256], psF[:, 0:256], [pkF], ["dbgF"])
                    CP("act", tmpP[1][:, 0:256], psB[:, 0:256], [pkB], ["dbgB"])
                    dbg_list.extend([("psF", tmpP[0][:, 0:256]), ("psB", tmpP[1][:, 0:256]), ("Mge", Mge), ("Mle", Mle), ("tz1", tz1), ("tz2", tz2)])
                    return finish()
                TT("dve", tz1, psF[:, 128:256], Mge, ALU.mult, [pkF, "Mge"], ["tz1"])
                TT("dve", tz2, psB[:, 128:256], Mle, ALU.mult, [pkB, "Mle"], ["tz2"])
                TT("dve", tz1, tz1, tz2, ALU.add, ["tz1", "tz2"], ["tz1"])
                TS("dve", tz2, ident_f, Dv[:, g:g + 1], None, ALU.mult, None, ["ident_f", "Dv", "tz2"], ["tz2"])
                TT("dve", Tz4[:, g, 2, :], tz1, tz2, ALU.add, ["tz1", "tz2"], ["Tz"])
                if stop == "g5":
                    return finish()
                if stop == "g9" and gl == 1:
                    return finish()
                if stop == "g10" and gl == 2:
                    return finish()
            if stop == "g6":
                return finish()
        p.barrier()
        if stop == "0b":
            return finish()
        A.top = persist_top

        w_ua = A.bf(8 * 512)
        w_ua3 = w_ua.rearrange("p (k n) -> p k n", k=8)
        bua_rep = A.f32(512)
        LD("pool", w_ua3, dr["w_in"][:, 0:512].rearrange("(k p) n -> p k n", p=128), ["w_ua"], "c4")
        LD("sp", bua_rep, dr["b_in"][0:512].partition_broadcast(128), ["bua"], "c5")
        V = A.bf(G * 256); V3 = V.rearrange("p (g s) -> p g s", g=G)
        m0 = A.top
        hT = A.bf(8 * 1024); hT3 = hT.rearrange("p (k n) -> p k n", k=8)
        Z = A.bf(G * 128); Z4 = Z.rearrange("p (g j h) -> p g j h", g=G, j=8)
        NXS = 6
        xs = [A.f32(1024) for _ in range(NXS)]
        xns = [A.bf(1024), A.bf(1024)]
        stats = [A.f32(16) for _ in range(NXS)]
        topX = A.top
        A.top = m0
        LL = A.f32(2 * G * 128); LL4 = LL.rearrange("p (r g c) -> p r g c", r=2, g=G)
        LLn = LL.rearrange("p (r j g k) -> p r j g k", r=2, g=G, j=8)
        Sin = A.bf(2 * G * 128); Sin4 = Sin.rearrange("p (r g c) -> p r g c", r=2, g=G)
        Yz = A.bf(8 * 128)
        Zo = A.bf(2 * 8 * 512); Zo5 = Zo.rearrange("p (i t g h) -> p i t g h", i=2, t=8, g=G)
        Gc = A.f32(2 * G * 17); Gc4 = Gc.rearrange("p (r g k) -> p r g k", r=2, g=G)
        ZoF = Zo.bitcast(F32)
        WTW = ZoF[:, 0:1024]
        Ts2 = [ZoF[:, 2048:2112], ZoF[:, 2112:2176]]
        topY = A.top
        A.top = m0
        Lc = A.f32(2 * G * 64); Lc5 = Lc.rearrange("p (r g b c) -> p r g b c", r=2, g=G, b=BPC)
        Sc = A.f32(2 * G * BPC); Sc4 = Sc.rearrange("p (r g b) -> p r g b", r=2, g=G)
        Tc1 = [A.f32(2 * G * BPC), A.f32(2 * G * BPC)]; Tc2 = [A.f32(2 * G * BPC), A.f32(2 * G * BPC)]
        A.top = max(topX, topY, A.top)
        stageA_top = A.top
        ucnt = [0]
        VKEYS = ["V%d_%d" % (h_, g_) for h_ in range(2) for g_ in range(4)]

        def ln_tile(src_ap, slot, modcol, hT_dst, width, tagp):
            i = ucnt[0]; ucnt[0] += 1
            slot = i % NXS
            xk = "xs%d" % slot
            stat = stats[slot]; sk = "stat%d" % slot
            xn = xns[i % 2]; xnk = "xn%d" % (i % 2)
            LD("sp" if i % 2 == 0 else "act", xs[slot], src_ap, [xk], "x%d" % slot, total=False)
            p.op("dve", lambda e: e.bn_stats(out=stat[:, 0:6], in_=xs[slot][:, 0:512]), [xk], [sk], dur=700.0)
            p.op("dve", lambda e: e.bn_stats(out=stat[:, 6:12], in_=xs[slot][:, 512:1024]), [xk], [sk], dur=700.0)
            p.op("dve", lambda e: e.bn_aggr(out=stat[:, 12:14], in_=stat[:, 0:12]), [sk], [sk + "mv"], dur=250.0)
            TS("pool", stat[:, 14:15], stat[:, 13:14], EPS, None, ALU.add, None, [sk + "mv"], [sk + "rs"])
            TT("pool", stat[:, 14:15], stat[:, 14:15], mhalf[:, 0:1], ALU.pow, [sk + "rs", "mhalf"], [sk + "rs"])
            TS("dve", xn, xs[slot], stat[:, 12:13], stat[:, 14:15], ALU.subtract, ALU.mult, [xk, sk + "mv", sk + "rs"], [xnk])
            pb = psb[i % 2]; pk = "psb%d" % (i % 2)
            for kt in range(8):
                TR(pb[:, kt * 128:(kt + 1) * 128], xn[:, kt * 128:(kt + 1) * 128], ident_bf, [xnk, "ident_bf"], [pk])
            for kt in range(8):
                if i % 2 == 0:
                    ACT(hT_dst(kt), pb[:, kt * 128:(kt + 1) * 128], AF.Identity, [pk, "ST"], [tagp],
                        scale=ST3[:, 8 + kt, modcol:modcol + 1], bias=ST3[:, kt, modcol:modcol + 1])
                else:
                    TS("dve", hT_dst(kt), pb[:, kt * 128:(kt + 1) * 128], ST3[:, 8 + kt, modcol:modcol + 1],
                       ST3[:, kt, modcol:modcol + 1], ALU.mult, ALU.add, [pk, "ST"], [tagp])

        def tile1024(srcs, modcols, Vdst, vtag="V"):
            for s in range(8):
                ln_tile(srcs[s], s % 2, modcols[s], lambda kt, s=s: hT3[:, kt, s * 128:(s + 1) * 128], 128, "hT%d" % s)
            hT4 = hT.rearrange("p (k c j) -> p k j c", k=8, j=8)
            for j in range(8):
                ps_ = psf[j % 4]; pk = "psf%d" % (j % 4)
                for kt in range(8):
                    MM(ps_[:, :], hT4[:, kt, j, :], w_ua3[:, kt, :], kt == 0, kt == 7, ["hT%d" % s_ for s_ in range(8)] + ["w_ua"], [pk])
                TT("dve", Z4[:, :, j, :], ps_[:, :].rearrange("p (g h) -> p g h", g=G), bua_rep.rearrange("p (g h) -> p g h", g=G),
                   ALU.add, [pk, "bua"], ["Z%d" % j])
            for g8 in range(4):
                pb = psb[g8 % 2]; pk = "psb%d" % (g8 % 2)
                for k in range(8):
                    g = g8 * 8 + k
                    TR(pb[:, k * 128:(k + 1) * 128], Z4[:, g, :, :].rearrange("p j h -> p (j h)"), ident_bf, ["Z%d" % j_ for j_ in range(8)] + ["ident_bf"], [pk])
                CP("act" if g8 % 2 == 0 else "dve", Vdst[:, g8 * 8:(g8 + 1) * 8, :], pb[:, :].rearrange("p (g s) -> p g s", g=8), [pk], ["%s_%d" % (vtag, g8)])

        srcs = [dr["ctx"][s * 128:(s + 1) * 128, :] for s in range(8)]
        Vc = V3[:, :, 0:128]
        tile1024(srcs, [4] * 8, Vc, "V0")
        p.barrier()
        Vc4 = V.rearrange("p (g s) -> p g s", g=G)[:, :, 0:128].rearrange("p g (c j) -> p g j c", j=2)
        for g4 in range(0, G, 4):
            ps_ = psf[(g4 // 4) % 4]; pk = "psf%d" % ((g4 // 4) % 4)
            for k in range(4):
                g = g4 + k
                for ri in range(2):
                    col = (k * 2 + ri) * 64
                    for J in range(2):
                        MM(ps_[:, col:col + 64], W_in5[:, g, J, ri, :], Vc4[:, g, J, :], J == 0, J == 1, ["W_in"] + VKEYS, [pk])
            CP("dve", Lc5[:, :, g4:g4 + 4, :, :].rearrange("p r g b c -> p g r (b c)"),
               ps_[:, :].rearrange("p (g r n) -> p g r n", g=4, r=2), [pk], ["Lc"])
        for (lo, hi, eng, order) in ((0, 64, "dve", list(range(16))), (64, 128, "pool", list(range(15, -1, -1)))):
            tg = "c%d" % lo
            a1 = bc(AA1_3[lo:hi], [hi - lo, 2, G, BPC], 3); a2 = bc(AA2_3[lo:hi], [hi - lo, 2, G, BPC], 3)
            S = Sc4[lo:hi]
            t1_ = Tc1[lo // 64][lo:hi].rearrange("p (r g b) -> p r g b", r=2, g=G)
            t2_ = Tc2[lo // 64][lo:hi].rearrange("p (r g b) -> p r g b", r=2, g=G)
            for n, c in enumerate(order):
                Lcur = Lc5[lo:hi, :, :, :, c]
                if n == 0:
                    CP(eng, S, Lcur, ["Lc"], ["S" + tg])
                    continue
                TT(eng, t1_, S, a1, ALU.mult, ["S" + tg, "AA1"], ["t1" + tg])
                TT(eng, t2_[:, 0], S[:, 1], a2[:, 0], ALU.mult, ["S" + tg, "AA2"], ["t2" + tg])
                TT(eng, t2_[:, 1], S[:, 0], a2[:, 1], ALU.mult, ["S" + tg, "AA2"], ["t2" + tg])
                TT(eng, t1_, t1_, t2_, ALU.add, ["t1" + tg, "t2" + tg], ["t1" + tg])
                TT(eng, S, t1_, Lcur, ALU.add, ["t1" + tg, "Lc"], ["S" + tg])
            CP(eng, H04[lo:hi], S, ["S" + tg], ["H0"])
        p.barrier()
        if stop == "C":
            return finish()

        for b in range(BPC):
            for half in range(2):
                srcs = [dr["x"][b, half * 1024 + s * 128: half * 1024 + (s + 1) * 128, :] for s in range(8)]
                tile1024(srcs, [b] * 8, V3[:, :, half * 128:(half + 1) * 128], "V%d" % half)
            p.barrier()
            Vj = V3.rearrange("p g (c j) -> p g j c", j=2)
            for g2 in range(0, G, 2):
                ps_ = psf[(g2 // 2) % 4]; pk = "psf%d" % ((g2 // 2) % 4)
                for k in range(2):
                    g = g2 + k
                    for ri in range(2):
                        col = (k * 2 + ri) * 128
                        for J in range(2):
                            MM(ps_[0:64, col:col + 128], W_in5[:, g, J, ri, 0:64], Vj[:, g, J, :], J == 0, J == 1, ["W_in"] + VKEYS, [pk])
                        for J in range(2):
                            MM(ps_[64:128, col:col + 128], W_in5[:, g, J, ri, 64:128], Vj[:, g, J, ::-1], J == 0, J == 1, ["W_in"] + VKEYS, [pk])
                for k in range(2):
                    CP("act" if (g2 // 2) % 2 == 0 else "dve", LLn[:, :, :, g2 + k, :],
                       ps_[:, k * 256:(k + 1) * 256].rearrange("p (r k j) -> p r j k", r=2, j=8), [pk], ["LL%d" % (g2 + k)])
            GS = 10
            LLKEYS = ["LL%d" % g_ for g_ in range(G)]
            chains = [(0, 128, True, "dve", "dve", 0, G, WTW)]
            for (lo, hi, fwd, eng, eng2, g0, g1, wt) in chains:
                tg = "s%d_%d" % (lo, g0)
                ng = g1 - g0
                key = "LL" + tg
                LLv = LLn[lo:hi, :, :, g0:g1, :].rearrange("p r j g k -> p r g k j")
                Sv = Sin4[lo:hi, :, g0:g1, :].rearrange("p r g (k j) -> p r g k j", j=8)
                Gv = Gc4[lo:hi, :, g0:g1, :]
                tw = wt[lo:hi, 0:2 * ng * 16].rearrange("p (r g k) -> p r g k", r=2, g=ng)

                def cmad(dst, src, m, t, shp, rkeys, wkeys, bcast, eng=eng):
                    a1 = APW1[m][lo:hi, :, g0:g1]; a2 = APW2[m][lo:hi, :, g0:g1]
                    if bcast:
                        a1 = bc(a1, shp, 3); a2 = bc(a2, shp, 3)
                    tk = "tw" + tg + ("" if bcast else "s")
                    TT(eng, t, src, a1, ALU.mult, rkeys + ["APW"], [tk])
                    TT(eng, dst, dst, t, ALU.add, [tk] + rkeys + wkeys, wkeys)
                    TT(eng, t[:, 0], src[:, 1], a2[:, 0], ALU.mult, rkeys + ["APW"], [tk])
                    TT(eng, t[:, 1], src[:, 0], a2[:, 1], ALU.mult, rkeys + ["APW"], [tk])
                    TT(eng, dst, dst, t, ALU.add, [tk] + rkeys + wkeys, wkeys)

                shp = [hi - lo, 2, ng, 16]
                js = list(range(1, 8)) if fwd else list(range(6, -1, -1))
                for j in js:
                    jp = j - 1 if fwd else j + 1
                    cmad(LLv[:, :, :, :, j], LLv[:, :, :, :, jp], 1, tw, shp, LLKEYS[g0:g1] + [key], [key], True)
                ts_ = Ts2[lo // 64][lo:hi, 0:2 * ng].rearrange("p (r g) -> p r g", r=2)
                if fwd:
                    CP(eng2, Gv[:, :, :, 0], H04[lo:hi, :, g0:g1, b], ["H0"], ["Gc" + tg])
                    CP(eng2, Gv[:, :, :, 1:17], LLv[:, :, :, :, 7], [key], ["Gc" + tg])
                    for k in range(16):
                        cmad(Gv[:, :, :, k + 1], Gv[:, :, :, k], 8, ts_, None, ["Gc" + tg], ["Gc" + tg], False, eng=eng2)
                else:
                    CP(eng2, Gv[:, :, :, 16], H04[lo:hi, :, g0:g1, b], ["H0"], ["Gc" + tg])
                    CP(eng2, Gv[:, :, :, 0:16], LLv[:, :, :, :, 0], [key], ["Gc" + tg])
                    for k in range(15, -1, -1):
                        cmad(Gv[:, :, :, k], Gv[:, :, :, k + 1], 8, ts_, None, ["Gc" + tg], ["Gc" + tg], False, eng=eng2)
                Gin = Gv[:, :, :, 0:16] if fwd else Gv[:, :, :, 1:17]
                skey = "Sin" + ("a0" if fwd else "a64")
                for j in range(8):
                    m = j if fwd else 7 - j
                    SvN = Sin4[:, :, g0:g1, :].rearrange("p r g (k j) -> p r g k j", j=8)
                    if m == 0:
                        CP(eng, SvN[0:64, :, :, :, j], Gin[0:64], ["Gc" + tg], [skey + tg])
                        CP(eng, SvN[64:128, :, :, ::-1, 7 - j], Gin[64:128], ["Gc" + tg], [skey + tg + "b"])
                        continue
                    jp = j - 1 if fwd else j + 1
                    Pj = LLv[:, :, :, :, jp]
                    a1 = bc(APW1[m][lo:hi, :, g0:g1], shp, 3); a2 = bc(APW2[m][lo:hi, :, g0:g1], shp, 3)
                    TT(eng, tw, Gin, a1, ALU.mult, ["Gc" + tg, "APW"], ["tw" + tg])
                    TT(eng, Pj, Pj, tw, ALU.add, ["tw" + tg, key], [key])
                    TT(eng, tw[:, 0], Gin[:, 1], a2[:, 0], ALU.mult, ["Gc" + tg, "APW"], ["tw" + tg])
                    TT(eng, tw[:, 1], Gin[:, 0], a2[:, 1], ALU.mult, ["Gc" + tg, "APW"], ["tw" + tg])
                    TT(eng, SvN[0:64, :, :, :, j], Pj[0:64], tw[0:64], ALU.add, ["tw" + tg, key], [skey + tg])
                    TT(eng, SvN[64:128, :, :, ::-1, 7 - j], Pj[64:128], tw[64:128], ALU.add, ["tw" + tg, key], [skey + tg + "b"])
            DELTA = {(0, 0): 2, (0, 1): 1, (1, 0): 0, (1, 1): 2}
            SINKEYS = ["Sina0s0_0", "Sina0s0_0b"]
            for g4 in range(0, G, 2):
                ps_ = psf[(g4 // 2) % 4]; pk = "psf%d" % ((g4 // 2) % 4)
                for k in range(2):
                    g = g4 + k
                    for I in range(2):
                        col = (k * 2 + I) * 128
                        o = ps_[:, col:col + 128]
                        MM(o, Tz4[:, g, DELTA[(I, 0)], :], Vj[:, g, 0, :], True, False, ["Tz"] + VKEYS, [pk])
                        MM(o, Tz4[:, g, DELTA[(I, 1)], :], Vj[:, g, 1, :], False, False, ["Tz"] + VKEYS, [pk])
                        MM(o, W_oR4[:, g, 8 * I:8 * I + 8, :].rearrange("p t h -> p (t h)"), Sin4[:, 0, g, :], False, False,
                           ["W_oR"] + SINKEYS, [pk])
                        MM(o, W_oI4[:, g, 8 * I:8 * I + 8, :].rearrange("p t h -> p (t h)"), Sin4[:, 1, g, :], False, True,
                           ["W_oI"] + SINKEYS, [pk])
                yk = "Yz%d" % ((g4 // 2) % 2)
                yz = Yz[:, ((g4 // 2) % 2) * 512:((g4 // 2) % 2 + 1) * 512]
                CP("act" if (g4 // 2) % 2 == 0 else "dve", yz, ps_[:, :], [pk], [yk])
                pb = psb[(g4 // 2) % 2]; pbk = "psb%d" % ((g4 // 2) % 2)
                for k in range(2):
                    for I in range(2):
                        q = k * 2 + I
                        TR(pb[:, q * 128:(q + 1) * 128], yz[:, q * 128:(q + 1) * 128], ident_bf, [yk, "ident_bf"], [pbk])
                for k in range(2):
                    CP("dve" if (g4 // 2) % 2 == 0 else "act", Zo5[:, :, :, g4 + k, :],
                       pb[:, k * 256:(k + 1) * 256].rearrange("p (i t h) -> p i t h", i=2, t=8), [pbk], ["Zo%d" % (g4 + k)])
            p.dma("sp", yscr[b].rearrange("(c it) ch -> c (it ch)", it=16), Zo, reads=["Zo%d" % g_ for g_ in range(G)], writes=["yscr"], stream="ys", total=False)
            p.barrier()
            if stop == "A1":
                return finish()
        if stop == "A":
            return finish()

        A.top = small_top
        NS = TB // 128
        NT = BPC * (L // TB)
        w_in2 = A.bf(8 * 4096); w_in23 = w_in2.rearrange("p (k n) -> p k n", k=8)
        w_glu = A.bf(4 * 512); w_glu3 = w_glu.rearrange("p (k n) -> p k n", k=4)
        w_a = A.bf(4 * 1024); w_a3 = w_a.rearrange("p (k n) -> p k n", k=4)
        w_b = A.bf(4 * 1024); w_b3 = w_b.rearrange("p (k n) -> p k n", k=4)
        w_o = A.bf(8 * 1024); w_o3 = w_o.rearrange("p (k n) -> p k n", k=8)
        wsT = A.bf(8 * 128); wsT3 = wsT.rearrange("p (g n) -> p g n", g=8)
        bcol = A.f32(32)
        gcol = A.f32(4)
        bs_t = A.f32(4 * 128); bs_t3 = bs_t.rearrange("p (q n) -> p q n", q=4)
        bvb_rep = A.f32(512); sg_rep = A.f32(512); sb_rep = A.f32(512)
        lng_rep = A.f32(1024); lnb_rep = A.f32(1024)
        gate_rep = [A.f32(1024), A.f32(1024)]
        ones_r = A.bf(128); bout_r = A.bf(1024)
        xn = A.bf(1024)
        hB = [A.bf(8 * TB), A.bf(8 * TB)]
        hB3 = [h.rearrange("p (k n) -> p k n", k=8) for h in hB]
        xB = [[A.f32(1024) for _ in range(NS)] for _ in range(2)]
        bufs = {}
        for nm in ("za", "ub", "zb", "gT", "gg", "yb"):
            bufs[nm] = A.bf(4 * TB).rearrange("p (k n) -> p k n", k=4)
        sga = A.bf(8 * TB).rearrange("p (k n) -> p k n", k=8)
        sgb = A.bf(8 * TB).rearrange("p (k n) -> p k n", k=8)
        mB = A.bf(8 * TB); mB3 = mB.rearrange("p (k n) -> p k n", k=8)
        vtm = A.bf(NS * 512); vtm3 = vtm.rearrange("p (s n) -> p s n", s=NS)
        ytm = A.bf(NS * 512); ytm3 = ytm.rearrange("p (s n) -> p s n", s=NS)
        vf = [A.f32(512), A.f32(512)]
        mt = [vf[0][:, 0:TB], vf[1][:, 0:TB]]
        rr = [A.f32(1024), A.f32(1024)]; ro = rr
        statX = A.f32(32); statV = A.f32(32); statO = A.f32(32)
        ws_st = rr[0].bitcast(BF16)[:, 0:1024]; ws_st3 = ws_st.rearrange("p (g m) -> p g m", g=8)
        for q in range(4):
            LD("pool", w_in23[:, 2 * q:2 * q + 2, :], dr["w_in"][q * 256:(q + 1) * 256, 512:DIN].rearrange("(k p) n -> p k n", p=128), ["w_in2"], "w0")
        LD("pool", w_glu3, dr["glu_w"].rearrange("(k p) n -> p k n", p=128), ["w_glu"], "w0")
        LD("pool", w_a3, dr["w_a"].rearrange("(k p) n -> p k n", p=128), ["w_a"], "w0")
        LD("pool", w_b3, dr["w_b"].rearrange("(k p) n -> p k n", p=128), ["w_b"], "w0")
        LD("pool", w_o3, dr["w_out"].rearrange("(k p) n -> p k n", p=128), ["w_o"], "w0")
        LD("pool", ws_st3, dr["sgu_w"].rearrange("g n m -> n g m"), ["ws_st"], "w0")
        LD("pool", bout_r[0:1, :], dr["b_out"].rearrange("(o n) -> o n", o=1), ["bout_r"], "w0")
        p.op("dve", lambda e: e.memset(ones_r[0:1, :], 1.0), [], ["ones_r"])
        LD("sp", bcol, dr["b_in"][512:DIN].rearrange("(t p) -> p t", p=128), ["bcol"], "w1", allow_slow_non_contiguous=True)
        LD("sp", gcol, dr["glu_b"].rearrange("(t p) -> p t", p=128), ["gcol"], "w1", allow_slow_non_contiguous=True)
        for q in range(4):
            for hh in range(2):
                LD("sp", bs_t3[hh * 64:(hh + 1) * 64, q, :], dr["sgu_b"][2 * q + hh].partition_broadcast(64), ["bs_t"], "w1")
        LD("sp", bvb_rep, dr["b_in"][1536:2048].partition_broadcast(128), ["bvb"], "w1")
        LD("sp", sg_rep, dr["sgu_ln_g"].partition_broadcast(128), ["sg"], "w1")
        LD("sp", sb_rep, dr["sgu_ln_b"].partition_broadcast(128), ["sb"], "w1")
        LD("sp", lng_rep, dr["ln_g"].partition_broadcast(128), ["lng"], "w1")
        LD("sp", lnb_rep, dr["ln_b"].partition_broadcast(128), ["lnb"], "w1")
        for g in range(8):
            TR(psb[0][:, g * 128:(g + 1) * 128], ws_st3[:, g, :], ident_bf, ["ws_st", "ident_bf"], ["psb0"])
        CP("dve", wsT, psb[0][:, :], ["psb0"], ["wsT"])
        p.barrier()

        def rstd_chain(stat, n, tag):
            TS("pool", stat[:, 20:20 + n], stat[:, 13:13 + 2 * n].rearrange("p (j t) -> p j t", t=2)[:, :, 0], EPS, None, ALU.add, None,
               ["mv" + tag], ["rs" + tag])
            TT("pool", stat[:, 20:20 + n], stat[:, 20:20 + n], mhalf[:, 0:n], ALU.pow, ["rs" + tag, "mhalf"], ["rs" + tag])

        def F0(i):
            b = i // (L // TB); tok0 = (i % (L // TB)) * TB
            par = i % 2
            for s in range(NS):
                xk = "xB%d_%d" % (par, s)
                LD("sp" if s % 2 == 0 else "act", xB[par][s], dr["x"][b, tok0 + s * 128:tok0 + (s + 1) * 128, :], [xk], "xb%d_%d" % (par, s), total=False)
                p.op("dve", lambda e, s=s: e.bn_stats(out=statX[:, 0:6], in_=xB[par][s][:, 0:512]), [xk], ["stX"])
                p.op("dve", lambda e, s=s: e.bn_stats(out=statX[:, 6:12], in_=xB[par][s][:, 512:1024]), [xk], ["stX"])
                p.op("dve", lambda e, s=s: e.bn_aggr(out=statX[:, 12 + 2 * s:14 + 2 * s], in_=statX[:, 0:12]), ["stX"], ["mvX"])
            rstd_chain(statX, NS, "X")
            for s in range(NS):
                xk = "xB%d_%d" % (par, s)
                TS("dve", xn, xB[par][s], statX[:, 12 + 2 * s:13 + 2 * s], statX[:, 20 + s:21 + s], ALU.subtract, ALU.mult, [xk, "mvX", "rsX"], ["xn"])
                pb = psb[s % 2]; pk = "psb%d" % (s % 2)
                for kt in range(8):
                    TR(pb[:, kt * 128:(kt + 1) * 128], xn[:, kt * 128:(kt + 1) * 128], ident_bf, ["xn", "ident_bf"], [pk])
                for kt in range(8):
                    dst = hB3[par][:, kt, s * 128:(s + 1) * 128]
                    if s % 2 == 0:
                        ACT(dst, pb[:, kt * 128:(kt + 1) * 128], AF.Identity, [pk, "ST"], ["hB%d_%d" % (par, s)],
                            scale=ST3[:, 8 + kt, b:b + 1], bias=ST3[:, kt, b:b + 1])
                    else:
                        TS("dve", dst, pb[:, kt * 128:(kt + 1) * 128], ST3[:, 8 + kt, b:b + 1], ST3[:, kt, b:b + 1],
                           ALU.mult, ALU.add, [pk, "ST"], ["hB%d_%d" % (par, s)])

        pcnt = [0]

        def proj_fm(h3, hk, col0, ntile, func, dst3, key):
            for t_ in range(ntile):
                q = pcnt[0] % 4; pcnt[0] += 1
                ps_ = psf[q]; pk = "psf%d" % q
                c0 = col0 + t_ * 128
                for kt in range(8):
                    MM(ps_[:, 0:TB], w_in23[:, kt, c0:c0 + 128], h3[:, kt, :], kt == 0, kt == 7, ["w_in2"] + hk, [pk])
                ACT(dst3[:, t_, :], ps_[:, 0:TB], func, [pk, "bcol"], [key], bias=bcol[:, c0 // 128:c0 // 128 + 1])

        oc = [0]

        def MAIN(i):
            b = i // (L // TB); tok0 = (i % (L // TB)) * TB
            par = i % 2
            h3 = hB3[par]; hk = ["hB%d_%d" % (par, s_) for s_ in range(NS)]
            gr = gate_rep[b % 2]; grk = "gate_rep%d" % (b % 2)
            if i % (L // TB) == 0:
                for hh in range(2):
                    MM(psf[4][:, :], sel[0:5, b * 128:(b + 1) * 128], grow[0:5, hh * 512:(hh + 1) * 512], True, True, ["sel", "grow"], ["psf4"])
                    CP("dve", gr[:, hh * 512:(hh + 1) * 512], psf[4][:, :], ["psf4"], [grk])
            for s in range(NS):
                LD("sp", ytm3[:, s, :], yscr[b, tok0 + s * 128:tok0 + (s + 1) * 128, :], ["ytm%d" % s], "yl%d" % s, total=False)
                pb = psb[s % 2]; pk = "psb%d" % (s % 2)
                for q in range(4):
                    TR(pb[:, q * 128:(q + 1) * 128], ytm3[:, s, q * 128:(q + 1) * 128], ident_bf, ["ytm%d" % s, "ident_bf"], [pk])
                ACT(bufs["gT"][:, :, s * 128:(s + 1) * 128], pb[:, 0:512].rearrange("p (q n) -> p q n", q=4), AF.Gelu_apprx_tanh, [pk], ["gT"])
            proj_fm(h3, hk, 512, 4, AF.Gelu_apprx_tanh, bufs["ub"], "ub")
            for s in range(NS):
                ps_ = psf[4 + s % 2]; pk = "psf%d" % (4 + s % 2)
                for kt in range(8):
                    MM(ps_[:, :], h3[:, kt, s * 128:(s + 1) * 128], w_in23[:, kt, 1024:1536], kt == 0, kt == 7, [hk[s], "w_in2"], [pk])
                TT("dve", vf[s], ps_[:, :], bvb_rep, ALU.add, [pk, "bvb"], ["vf%d" % s])
                ACT(vf[s], vf[s], AF.Gelu_apprx_tanh, ["vf%d" % s], ["vf%d" % s])
                p.op("dve", lambda e, s=s: e.bn_stats(out=statV[:, 0:6], in_=vf[s]), ["vf%d" % s], ["stV"])
                p.op("dve", lambda e, s=s: e.bn_aggr(out=statV[:, 12 + 2 * s:14 + 2 * s], in_=statV[:, 0:6]), ["stV"], ["mvV"])
            proj_fm(h3, hk, 1536, 4, AF.Silu, bufs["zb"], "zb")
            proj_fm(h3, hk, 0, 4, AF.Silu, bufs["za"], "za")
            rstd_chain(statV, NS, "V")
            for s in range(NS):
                TS("dve", vf[s], vf[s], statV[:, 12 + 2 * s:13 + 2 * s], statV[:, 20 + s:21 + s], ALU.subtract, ALU.mult,
                   ["vf%d" % s, "mvV", "rsV"], ["vf%d" % s])
                TT("dve", vf[s], vf[s], sg_rep, ALU.mult, ["vf%d" % s, "sg"], ["vf%d" % s])
                TT("dve", vtm3[:, s, :], vf[s], sb_rep, ALU.add, ["vf%d" % s, "sb"], ["vtm"])
            proj_fm(h3, hk, 3072, 8, AF.Sigmoid, sgb, "sgb")
            proj_fm(h3, hk, 2048, 8, AF.Sigmoid, sga, "sga")
            for t_ in range(4):
                q = pcnt[0] % 4; pcnt[0] += 1
                ps_ = psf[q]; pk = "psf%d" % q
                for kt in range(4):
                    MM(ps_[:, 0:TB], w_glu3[:, kt, t_ * 128:(t_ + 1) * 128], bufs["gT"][:, kt, :], kt == 0, kt == 3, ["w_glu", "gT"], [pk])
                ACT(bufs["gg"][:, t_, :], ps_[:, 0:TB], AF.Sigmoid, [pk, "gcol"], ["gg"], bias=gcol[:, t_:t_ + 1])
            TT("pool", bufs["ub"], bufs["ub"], bufs["zb"], ALU.mult, ["ub", "zb"], ["ub"])
            for s in range(NS):
                for q in range(4):
                    qq = pcnt[0] % 4; pcnt[0] += 1
                    pq = psf[qq]; pqk = "psf%d" % qq
                    for hh in range(2):
                        g = 2 * q + hh
                        MM(pq[hh * 64:(hh + 1) * 64, 0:128], vtm3[:, s, g * 64:(g + 1) * 64], wsT3[:, g, :], True, True, ["vtm", "wsT"], [pqk])
                    TT("dve", bufs["yb"][:, q, s * 128:(s + 1) * 128], pq[:, 0:128], bs_t3[:, q, :], ALU.add, [pqk, "bs_t"], ["yb"])
            TT("dve", bufs["yb"], bufs["yb"], bufs["ub"], ALU.mult, ["yb", "ub"], ["yb"])
            TT("pool", bufs["gg"], bufs["gg"], bufs["gT"], ALU.mult, ["gg", "gT"], ["gg"])
            for t_ in range(8):
                q = pcnt[0] % 4; pcnt[0] += 1
                ps_ = psf[q]; pk = "psf%d" % q
                for kt in range(4):
                    MM(ps_[:, 0:TB], w_b3[:, kt, t_ * 128:(t_ + 1) * 128], bufs["yb"][:, kt, :], kt == 0, kt == 3, ["w_b", "yb"], [pk])
                TT("dve", mB3[:, t_, :], ps_[:, 0:TB], sgb[:, t_, :], ALU.mult, [pk, "sgb"], ["mB%d" % t_])
            TT("dve", bufs["gg"], bufs["gg"], bufs["za"], ALU.mult, ["gg", "za"], ["gg"])
            for t_ in range(8):
                q = pcnt[0] % 4; pcnt[0] += 1
                ps_ = psf[q]; pk = "psf%d" % q
                for kt in range(4):
                    MM(ps_[:, 0:TB], w_a3[:, kt, t_ * 128:(t_ + 1) * 128], bufs["gg"][:, kt, :], kt == 0, kt == 3, ["w_a", "gg"], [pk])
                TT("dve", mt[t_ % 2], ps_[:, 0:TB], sga[:, t_, :], ALU.mult, [pk, "sga"], ["vf%d" % (t_ % 2)])
                TT("pool" if t_ % 2 == 0 else "dve", mB3[:, t_, :], mB3[:, t_, :], mt[t_ % 2], ALU.add, ["vf%d" % (t_ % 2), "mB%d" % t_], ["mB%d" % t_])
            for s in range(NS):
                o_i = oc[0] % 2; oc[0] += 1
                rk = "rr%d" % o_i; ok = rk
                for hh in range(2):
                    ps_ = psf[4 + hh]; pk = "psf%d" % (4 + hh)
                    MM(ps_[:, :], ones_r[0:1, :], bout_r[0:1, hh * 512:(hh + 1) * 512], True, False, ["ones_r", "bout_r"], [pk])
                    for kt in range(8):
                        MM(ps_[:, :], mB3[:, kt, s * 128:(s + 1) * 128], w_o3[:, kt, hh * 512:(hh + 1) * 512], False, kt == 7, ["mB%d" % kt, "w_o"], [pk])
                    TT("dve", rr[o_i][:, hh * 512:(hh + 1) * 512], ps_[:, :], gr[:, hh * 512:(hh + 1) * 512], ALU.mult, [pk, grk], [rk, rk + "a", rk + "b"])
                STT(rr[o_i], xB[par][s], ALPHA, rr[o_i], ALU.mult, ALU.add, ["xB%d_%d" % (par, s), rk], [rk])
                p.op("dve", lambda e, o_i=o_i: e.bn_stats(out=statO[:, 0:6], in_=rr[o_i][:, 0:512]), [rk], ["stO"])
                p.op("dve", lambda e, o_i=o_i: e.bn_stats(out=statO[:, 6:12], in_=rr[o_i][:, 512:1024]), [rk], ["stO"])
                p.op("dve", lambda e: e.bn_aggr(out=statO[:, 12:14], in_=statO[:, 0:12]), ["stO"], ["mvO"])
                rstd_chain(statO, 1, "O")
                TS("dve", ro[o_i], rr[o_i], statO[:, 12:13], statO[:, 20:21], ALU.subtract, ALU.mult, [rk, "mvO", "rsO"], [ok])
                TT("pool", ro[o_i][:, 0:512], ro[o_i][:, 0:512], lng_rep[:, 0:512], ALU.mult, [ok, "lng"], [ok + "a"])
                TT("dve", ro[o_i][:, 512:1024], ro[o_i][:, 512:1024], lng_rep[:, 512:1024], ALU.mult, [ok, "lng"], [ok + "b"])
                TT("pool", ro[o_i][:, 0:512], ro[o_i][:, 0:512], lnb_rep[:, 0:512], ALU.add, [ok + "a", "lnb"], [ok + "a"])
                TT("dve", ro[o_i][:, 512:1024], ro[o_i][:, 512:1024], lnb_rep[:, 512:1024], ALU.add, [ok + "b", "lnb"], [ok + "b"])
                p.dma("sp", out_d[b, tok0 + s * 128:tok0 + (s + 1) * 128, :], ro[o_i], reads=[ok, ok + "a", ok + "b"],
                      writes=["out%d" % o_i], stream="o%d" % o_i, total=False)

        F0(0)
        for i in range(NT):
            if i + 1 < NT:
                F0(i + 1)
            MAIN(i)
        p.wait_all("sp", ["out0", "out1"])
        p.wait_all("act", ["out0", "out1"])
        p.emit(st)
        print("ops:", {e: len(p.ops[e]) for e in ENGS}, "arena top", A.top, "est_us", p.est_ns / 1e3)
    return nc


_CACHE = {}


def kernel(**inputs):
    f32 = lambda a: np.ascontiguousarray(np.asarray(a, dtype=np.float32))
    if "nc" not in _CACHE:
        _CACHE["nc"] = build_program()
    nc = _CACHE["nc"]
    x = f32(inputs["x"]); c = f32(inputs["c"]); ctx = f32(inputs["ctx"]); c_ctx = f32(inputs["c_ctx"])
    shared = dict(
        w_mod=f32(inputs["w_mod"][0]), b_mod=f32(inputs["b_mod"][0]), w_in=f32(inputs["w_in"][0]), b_in=f32(inputs["b_in"][0]),
        lam_re=f32(inputs["ssm_lam_re"][0]), lam_im=f32(inputs["ssm_lam_im"][0]), log_dt=f32(inputs["ssm_log_dt"][0]),
        b_re=f32(inputs["ssm_b_re"][0]), b_im=f32(inputs["ssm_b_im"][0]), c_re=f32(inputs["ssm_c_re"][0]), c_im=f32(inputs["ssm_c_im"][0]),
        ssm_d=f32(inputs["ssm_d"][0]), glu_w=f32(inputs["glu_w"][0]), glu_b=f32(inputs["glu_b"][0]),
        sgu_ln_g=f32(inputs["sgu_ln_g"][0]), sgu_ln_b=f32(inputs["sgu_ln_b"][0]), sgu_w=f32(inputs["sgu_w"][0]), sgu_b=f32(inputs["sgu_b"][0]),
        w_a=f32(inputs["w_branch_a"][0]), w_b=f32(inputs["w_branch_b"][0]), w_out=f32(inputs["w_out"][0]), b_out=f32(inputs["b_out"][0]),
        ln_g=f32(inputs["ln_g"][0]), ln_b=f32(inputs["ln_b"][0]),
    )
    shared.update(host_consts())
    in_maps = []
    for i in range(NCORES):
        sl = slice(i * BPC, (i + 1) * BPC)
        m = dict(shared)
        m["x"] = np.ascontiguousarray(x[sl])
        m["c_all"] = np.ascontiguousarray(np.concatenate([c[sl], c_ctx[None, :]], axis=0))
        m["ctx"] = np.ascontiguousarray(ctx[sl].reshape(BPC * CTX, D))
        in_maps.append(m)
    res = run_bass_kernel_spmd(nc, in_maps, core_ids=list(range(NCORES)))
    out = np.concatenate([np.asarray(r["out"]) for r in res.results], axis=0)
    return out.astype(np.float32)
```

```python
import math
import numpy as np
from contextlib import ExitStack
import concourse.bass as bass
import concourse.mybir as mybir
from concourse.bass_utils import run_bass_kernel_spmd

F32 = mybir.dt.float32
BF16 = mybir.dt.bfloat16
I32 = mybir.dt.int32
ALU = mybir.AluOpType
AF = mybir.ActivationFunctionType

ENGS = ("sp", "act", "dve", "pool", "pe")
NCORES = 8
BPC = 4
L = 2048
D = 1024
CTX = 256
DIN = 4608
G = 32
TWO_PI = 2.0 * math.pi
ALPHA = 2.0 ** 0.25
EPS = 1e-6
TB = 256


class Prog:
    WINDOW = 64
    XLAT = 300.0
    SLAT = 120.0
    TABLE_NS = 1300.0

    def __init__(self, nc):
        self.nc = nc
        self.all = []
        self.last_w = {}
        self.readers = {}
        self.dma_total_streams = set()

    def _deps_for(self, reads, writes):
        deps = []
        for k in reads:
            if k in self.last_w:
                deps.append(self.last_w[k])
        for k in writes:
            if k in self.last_w:
                deps.append(self.last_w[k])
            deps.extend(self.readers.get(k, []))
        return deps

    def _commit(self, oid, reads, writes):
        for k in reads:
            self.readers.setdefault(k, []).append(oid)
        for k in writes:
            self.last_w[k] = oid
            self.readers[k] = []

    def op(self, eng, fn, reads=(), writes=(), dur=300.0, tab=None):
        if eng != "pe":
            writes = list(writes) + [k for k in reads if k.startswith("ps") and k not in writes]
        deps = sorted(set(self._deps_for(reads, writes)))
        oid = len(self.all)
        self.all.append(dict(id=oid, eng=eng, fn=fn, deps=deps, dma=None, dur=dur, barrier=False, tab=tab))
        self._commit(oid, reads, writes)

    def dma(self, eng, out, in_, reads=(), writes=(), stream=None, total=False, dur=3000.0, **kw):
        deps = sorted(set(self._deps_for(reads, writes)))
        if total:
            self.dma_total_streams.add(stream)
            deps = [d for d in deps if not (self.all[d]["dma"] == stream)]
        oid = len(self.all)
        self.all.append(dict(id=oid, eng=eng, fn=lambda e: e.dma_start(out=out, in_=in_, **kw), deps=deps, dma=stream, dur=dur, barrier=False))
        self._commit(oid, reads, writes)

    def wait_all(self, eng, keys):
        deps = sorted(set(self.last_w[k] for k in keys if k in self.last_w))
        self.all.append(dict(id=len(self.all), eng=eng, fn=None, deps=deps, dma=None, dur=10.0, barrier=False))

    def barrier(self):
        self.all.append(dict(id=len(self.all), eng=None, fn=None, deps=[], dma=None, dur=0.0, barrier=True))
        self.last_w = {}
        self.readers = {}

    def _schedule_segment(self, seg, t_base):
        fin = {}
        feng = {}
        cur_tab = [None]
        pend = {e: [o for o in seg if o["eng"] == e] for e in ENGS}
        head = {e: 0 for e in ENGS}
        done = set()
        free = {e: t_base for e in ENGS}
        seg_ids = set(o["id"] for o in seg)
        order = []
        remaining = len(seg)
        W = self.WINDOW
        while remaining:
            best = None
            for e in ENGS:
                lst = pend[e]
                h = head[e]
                cnt = 0
                j = h
                while j < len(lst) and cnt < W:
                    o = lst[j]
                    j += 1
                    if o["id"] in done:
                        continue
                    cnt += 1
                    ok = True
                    st = free[e]
                    for d in o["deps"]:
                        if d in seg_ids:
                            if d not in fin:
                                ok = False
                                break
                            fd = fin[d] + (self.SLAT if feng[d] == e else self.XLAT)
                            if e == "pe" and feng[d] == "pe":
                                fd = fin[d] - 200.0
                            if fd > st:
                                st = fd
                    if not ok:
                        continue
                    pen = 0.0
                    if e == "act" and o.get("tab") is not None and o["tab"] != cur_tab[0]:
                        pen = self.TABLE_NS
                    key = (st + pen, o["id"], st)
                    if best is None or key < best[0]:
                        best = (key, e, o)
                    if st + pen <= free[e]:
                        break
            assert best is not None, "scheduler stuck"
            (stp, _, _st), e, o = best
            st = stp
            issue = 60.0 if o["dma"] is not None else o["dur"]
            if e == "act" and o.get("tab") is not None:
                cur_tab[0] = o["tab"]
            free[e] = st + issue
            fin[o["id"]] = st + o["dur"]
            feng[o["id"]] = e
            done.add(o["id"])
            order.append(o)
            remaining -= 1
            lst = pend[e]
            while head[e] < len(lst) and lst[head[e]]["id"] in done:
                head[e] += 1
        t_end = max([t_base] + list(fin.values()))
        return order, t_end

    def emit(self, stack):
        nc = self.nc
        segs = [[]]
        for o in self.all:
            if o["barrier"]:
                segs.append([])
            else:
                segs[-1].append(o)
        seq = []
        t = 0.0
        for si, seg in enumerate(segs):
            if si > 0:
                seq.append(None)
            order, t = self._schedule_segment(seg, t)
            seq.extend(order)
        self.est_ns = t
        streams = {e: [] for e in ENGS}
        pos = {}
        dma_rank = {}
        dma_cnt = {}
        last_compute = {e: None for e in ENGS}
        for o in seq:
            if o is None:
                deps = [v for v in last_compute.values() if v is not None]
                dmas = dict(dma_cnt)
                for e in ENGS:
                    streams[e].append(dict(fn=None, deps=list(deps), dma=None, dma_totals=dmas, id=-1))
                continue
            e = o["eng"]
            pos[o["id"]] = (e, len(streams[e]))
            streams[e].append(o)
            if o["dma"] is not None:
                dma_cnt[o["dma"]] = dma_cnt.get(o["dma"], 0) + 1
                dma_rank[o["id"]] = dma_cnt[o["dma"]]
            elif o["fn"] is not None:
                last_compute[e] = o["id"]
        self.dma_cnt = dma_cnt
        by_id = {o["id"]: o for o in self.all}
        target = {e: [False] * len(streams[e]) for e in ENGS}
        for e in ENGS:
            for o in streams[e]:
                for d in o["deps"]:
                    od = by_id[d]
                    if od["dma"] is None and od["fn"] is not None and not (od["eng"] == "pe" and e == "pe"):
                        pe_, pi_ = pos[d]
                        target[pe_][pi_] = True
        cum = {}
        for e in ENGS:
            c = 0
            cum[e] = []
            for i in range(len(streams[e])):
                if target[e][i]:
                    c += 1
                cum[e].append(c)
        sems = {e: stack.enter_context(nc.semaphore("s_" + e)) for e in ENGS}
        dsems = {s: stack.enter_context(nc.semaphore("d_" + s)) for s in dma_cnt}
        block = stack.enter_context(nc.Block())
        self.ops = streams

        def run(ename, e):
            seen = {}
            for i, o in enumerate(streams[ename]):
                waits = {}
                for d in o["deps"]:
                    od = by_id[d]
                    if od["dma"] is not None:
                        key = ("d", od["dma"])
                        n = dma_cnt[od["dma"]] if od["dma"] in self.dma_total_streams else dma_rank[d]
                        val = 16 * n
                    elif od["fn"] is None:
                        continue
                    else:
                        if od["eng"] == "pe" and ename == "pe":
                            continue
                        pe_, pi_ = pos[d]
                        key = ("e", pe_)
                        val = cum[pe_][pi_]
                    if val > waits.get(key, 0):
                        waits[key] = val
                if o.get("dma_totals"):
                    for s_, n_ in o["dma_totals"].items():
                        key = ("d", s_)
                        if 16 * n_ > waits.get(key, 0):
                            waits[key] = 16 * n_
                for key, val in waits.items():
                    if seen.get(key, 0) >= val:
                        continue
                    seen[key] = val
                    e.wait_ge(sems[key[1]] if key[0] == "e" else dsems[key[1]], val)
                if o["fn"] is None:
                    continue
                ins = o["fn"](e)
                if o["dma"] is not None:
                    ins.then_inc(dsems[o["dma"]], 16)
                elif target[ename][i]:
                    ins.then_inc(sems[ename], 1)

        @block.sync
        def _(e):
            run("sp", e)

        @block.scalar
        def _(e):
            run("act", e)

        @block.vector
        def _(e):
            run("dve", e)

        @block.gpsimd
        def _(e):
            run("pool", e)

        @block.tensor
        def _(e):
            run("pe", e)


class Arena:
    def __init__(self, t):
        self.f = t[:, :]
        self.b = t[:, :].bitcast(BF16)
        self.i = t[:, :].bitcast(I32)
        self.top = 0
        self.cap = t.shape[1]

    def f32(self, n):
        off = self.top
        self.top += n
        assert self.top <= self.cap, ("arena overflow", self.top, self.cap)
        return self.f[:, off:off + n]

    def i32(self, n):
        off = self.top
        self.top += n
        assert self.top <= self.cap, ("arena overflow", self.top, self.cap)
        return self.i[:, off:off + n]

    def bf(self, n):
        n2 = (n + 1) // 2
        off = self.top
        self.top += n2
        assert self.top <= self.cap, ("arena overflow", self.top, self.cap)
        return self.b[:, 2 * off:2 * off + n]


INPUT_SPECS = [
    ("x", [BPC, L, D]), ("c_all", [BPC + 1, D]), ("ctx", [BPC * CTX, D]),
    ("w_mod", [D, 3 * D]), ("b_mod", [3 * D]), ("w_in", [D, DIN]), ("b_in", [DIN]),
    ("lam_re", [2, G, 64]), ("lam_im", [2, G, 64]), ("log_dt", [2, G]),
    ("b_re", [2, G, 64, 16]), ("b_im", [2, G, 64, 16]), ("c_re", [2, G, 16, 64]), ("c_im", [2, G, 16, 64]),
    ("ssm_d", [512]), ("glu_w", [512, 512]), ("glu_b", [512]), ("sgu_ln_g", [512]), ("sgu_ln_b", [512]),
    ("sgu_w", [8, 128, 128]), ("sgu_b", [8, 128]), ("w_a", [512, D]), ("w_b", [512, D]),
    ("w_out", [D, D]), ("b_out", [D]), ("ln_g", [D]), ("ln_b", [D]),
    ("k_ident", [128, 128]), ("k_ein", [128, 24]), ("k_eout", [128, 25]),
    ("k_mge", [128, 128]), ("k_mle", [128, 128]), ("k_sel", [5, 5 * 128]),
]


def host_consts():
    ident = np.eye(128, dtype=np.float32)
    ein = np.zeros((128, 24), np.float32)
    eout = np.zeros((128, 25), np.float32)
    for i in range(16):
        ein[0:64, i] = 15 - i
        ein[64:128, i] = i
    for i in range(16, 24):
        ein[0:64, i] = -(i - 16)
    for i in range(17):
        eout[0:64, i] = i
        eout[64:128, i] = 17 - i
    for i in range(17, 25):
        eout[64:128, i] = -(i - 17)
    jj = np.arange(128) // 16
    mge = (jj[None, :] >= jj[:, None]).astype(np.float32)
    mle = (jj[None, :] <= jj[:, None]).astype(np.float32)
    sel = np.zeros((5, 5, 128), np.float32)
    for b in range(5):
        sel[b, b, :] = 1.0
    return dict(k_ident=ident, k_ein=ein, k_eout=eout, k_mge=mge, k_mle=mle, k_sel=sel.reshape(5, 640))


def build_program(stop=None):
    nc = bass.Bass("TRN2", target_bir_lowering=False)
    dr = {}
    for name, shape in INPUT_SPECS:
        dr[name] = nc.dram_tensor(name, shape, F32, kind="ExternalInput").ap()
    out_d = nc.dram_tensor("out", [BPC, L, D], F32, kind="ExternalOutput").ap()
    yscr = nc.dram_tensor("yscr", [BPC, L, 512], BF16, kind="Internal").ap()

    st = ExitStack()
    with st:
        arena_t = st.enter_context(nc.sbuf_tensor("arena", [128, 52736], F32))
        psf = [st.enter_context(nc.psum_tensor("psf%d" % i, [128, 512], F32)) for i in range(6)]
        psb = [st.enter_context(nc.psum_tensor("psb%d" % i, [128, 1024], BF16)) for i in range(2)]
        A = Arena(arena_t)
        p = Prog(nc)

        dbg_list = []

        def finish():
            p.barrier()
            col = 0
            row = 0
            for nm_, ap_ in dbg_list:
                w_ = ap_.shape[1]
                if col + w_ > 1024:
                    col = 0
                    row += 128
                p.dma("sp", out_d[0, row:row + ap_.shape[0], col:col + w_], ap_, stream="dbg", total=True)
                print("DBG", nm_, row, col, ap_.shape)
                col += w_
            p.barrier()
            p.emit(st)
            print("ops:", {e: len(p.ops[e]) for e in ENGS}, "arena top", A.top, "stop", stop, "est_us", p.est_ns / 1e3)
            return nc

        def nfree(ap):
            n = 1
            for d_ in ap.shape[1:]:
                n *= d_
            return n

        def edur(eng, ap):
            n = nfree(ap)
            if eng == "act":
                return 330.0 + 0.75 * n
            if eng == "dve":
                return 150.0 + 1.05 * n
            if eng == "pool":
                return 200.0 + 2.3 * n
            return 300.0

        TABS = {AF.Gelu_apprx_tanh: "gelu", AF.Silu: "silu", AF.Sigmoid: "sigmoid", AF.Sqrt: "sqrt", AF.Exp: "exp", AF.Sin: "sin"}

        def ACT(out, in_, func, reads, writes, scale=1.0, bias=0.0):
            p.op("act", lambda e: e.activation(out=out, in_=in_, func=func, scale=scale, bias=bias), reads, writes, dur=edur("act", out),
                 tab=TABS.get(func))

        def TT(eng, out, a, b, op, reads, writes):
            p.op(eng, lambda e: e.tensor_tensor(out=out, in0=a, in1=b, op=op), reads, writes, dur=edur(eng, out))

        def TS(eng, out, a, s1, s2, op0, op1, reads, writes):
            if s2 is None:
                p.op(eng, lambda e: e.tensor_scalar(out=out, in0=a, scalar1=s1, scalar2=None, op0=op0), reads, writes, dur=edur(eng, out))
            else:
                p.op(eng, lambda e: e.tensor_scalar(out=out, in0=a, scalar1=s1, scalar2=s2, op0=op0, op1=op1), reads, writes, dur=edur(eng, out))

        def STT(out, a, s, b, op0, op1, reads, writes):
            p.op("dve", lambda e: e.scalar_tensor_tensor(out=out, in0=a, scalar=s, in1=b, op0=op0, op1=op1), reads, writes, dur=edur("dve", out))

        def CP(eng, out, in_, reads, writes):
            if eng == "act":
                ACT(out, in_, AF.Copy, reads, writes)
            else:
                p.op(eng, lambda e: e.tensor_copy(out=out, in_=in_), reads, writes, dur=edur(eng, out))

        def MM(out, lhsT, rhs, start, stop, reads, writes):
            p.op("pe", lambda e: e.matmul(out, lhsT=lhsT, rhs=rhs, start=start, stop=stop), reads, writes,
                 dur=(max(nfree(rhs), 64) / 2.2 + 10.0) * (4.0 if rhs.dtype == F32 else 1.0))

        def TR(out, in_, ident, reads, writes):
            p.op("pe", lambda e: e.transpose(out, in_, ident), reads, writes, dur=70.0 * (4.0 if in_.dtype == F32 else 1.0))

        def LD(eng, out, in_, writes, stream, reads=(), total=True, **kw):
            nb = nfree(out) * out.shape[0] * (2 if out.dtype == BF16 else 4)
            p.dma(eng, out, in_, reads=reads, writes=writes, stream=stream, total=total, dur=2500.0 + nb / 100.0, **kw)

        def bc(ap, shape, axis):
            return ap.unsqueeze(axis).to_broadcast(shape)

        ident_bf = A.bf(128)
        ident_f = A.f32(128)
        ST = A.f32(16 * 5)
        ST3 = ST.rearrange("p (i b) -> p i b", b=5)
        grow = A.f32(1024)
        sel = A.f32(640)
        mhalf = A.f32(2)
        p.op("dve", lambda e: e.memset(mhalf, -0.5), [], ["mhalf"])
        small_top = A.top
        W_in = A.bf(G * 2 * 2 * 128)
        W_in5 = W_in.rearrange("p (g j r m) -> p g j r m", g=G, j=2, r=2)
        W_oR = A.bf(G * 16 * 16)
        W_oI = A.bf(G * 16 * 16)
        W_oR4 = W_oR.rearrange("p (g t h) -> p g t h", g=G, t=16)
        W_oI4 = W_oI.rearrange("p (g t h) -> p g t h", g=G, t=16)
        Tz = A.bf(G * 3 * 128)
        Tz4 = Tz.rearrange("p (g d m) -> p g d m", g=G, d=3)
        AA1 = A.f32(64)
        AA2 = A.f32(64)
        AA1_3 = AA1.rearrange("p (r g) -> p r g", r=2)
        AA2_3 = AA2.rearrange("p (r g) -> p r g", r=2)
        APW1 = [None] + [A.f32(64).rearrange("p (r g) -> p r g", r=2) for _ in range(8)]
        APW2 = [None] + [A.f32(64).rearrange("p (r g) -> p r g", r=2) for _ in range(8)]
        H0 = A.f32(2 * G * BPC)
        H04 = H0.rearrange("p (r g b) -> p r g b", r=2, g=G)
        persist_top = A.top

        LD("pool", ident_bf, dr["k_ident"], ["ident_bf"], "c0")
        LD("sp", ident_f, dr["k_ident"], ["ident_f"], "c1")
        LD("sp", sel[0:5, :], dr["k_sel"], ["sel"], "c1")

        modrow = A.f32(3072)
        scT = A.f32(40)
        scT3 = scT.rearrange("p (k b) -> p k b", k=8)
        wm = [A.f32(3072), A.f32(3072)]
        bmod_rep = A.f32(3072)
        for b_ in range(BPC + 1):
            LD("sp", scT3[:, :, b_], dr["c_all"][b_].rearrange("(k p) -> p k", p=128), ["scT"], "c1", allow_slow_non_contiguous=True)
        LD("sp", bmod_rep[0:5, :], dr["b_mod"].partition_broadcast(5), ["bmod"], "c1")
        ACT(scT, scT, AF.Silu, ["scT"], ["scT"])
        for kt in range(8):
            LD("sp" if kt % 2 == 0 else "act", wm[kt % 2], dr["w_mod"][kt * 128:(kt + 1) * 128, :], ["wm%d" % (kt % 2)],
               "wm%d" % (kt % 2), total=False)
            for cb in range(6):
                MM(psf[cb][0:5, :], scT3[:, kt, :], wm[kt % 2][:, cb * 512:(cb + 1) * 512], kt == 0, kt == 7,
                   ["scT", "wm%d" % (kt % 2)], ["psf%d" % cb])
        for cb in range(6):
            TT("dve", modrow[0:5, cb * 512:(cb + 1) * 512], psf[cb][0:5, :], bmod_rep[0:5, cb * 512:(cb + 1) * 512], ALU.add,
               ["psf%d" % cb, "bmod"], ["modrow"])
        for i in range(16):
            TR(psf[0][:, i * 5:(i + 1) * 5], modrow[0:5, i * 128:(i + 1) * 128], ident_f[0:5, 0:5], ["modrow", "ident_f"], ["psf0"])
        CP("dve", ST, psf[0][:, 0:80], ["psf0"], ["ST"])
        TS("dve", ST[:, 40:80], ST[:, 40:80], 1.0, None, ALU.add, None, ["ST"], ["ST"])
        CP("dve", grow[0:5, :], modrow[0:5, 2048:3072], ["modrow"], ["grow"])
        p.barrier()
        if stop == "0a":
            return finish()
        A.top = persist_top

        import os as _os2
        for _i in range(int(_os2.environ.get("PADDVE", "0"))):
            p.op("dve", lambda e: e.memset(ST[:, 0:1], 0.0) if False else e.tensor_copy(out=modrow[0:5, 0:8], in_=modrow[0:5, 8:16]), [], [])
        LR = A.f32(G); LI = A.f32(G); DTl = A.f32(G); RHO = A.f32(G); TH = A.f32(G)
        Lst = A.f32(256)
        Lst4 = Lst.rearrange("g (a d q) -> g a d q", a=2, d=2)
        for a_, nm in enumerate(("lam_re", "lam_im")):
            for d_ in range(2):
                LD("sp", Lst4[0:32, a_, d_, :], dr[nm][d_], ["Lst"], "c2")
        for d_ in range(2):
            LD("sp", DTl[d_ * 64:(d_ + 1) * 64, :], dr["log_dt"][d_].partition_broadcast(64), ["DTl"], "c2")
        Ein = A.f32(24); Eout = A.f32(25)
        LD("sp", Ein, dr["k_ein"], ["Ein"], "c2")
        LD("sp", Eout, dr["k_eout"], ["Eout"], "c2")
        Mge = A.f32(128); Mle = A.f32(128)
        LD("sp", Mge, dr["k_mge"], ["Mge"], "c2")
        LD("sp", Mle, dr["k_mle"], ["Mle"], "c2")
        TR(psf[0][:, 0:32], Lst[0:32, 0:128], ident_f[0:32, 0:32], ["Lst", "ident_f"], ["psf0"])
        TR(psf[0][:, 32:64], Lst[0:32, 128:256], ident_f[0:32, 0:32], ["Lst", "ident_f"], ["psf0"])
        CP("dve", LR, psf[0][:, 0:32], ["psf0"], ["LR"])
        CP("dve", LI, psf[0][:, 32:64], ["psf0"], ["LI"])
        ACT(DTl, DTl, AF.Exp, ["DTl"], ["DTl"])
        TT("dve", RHO, LR, DTl, ALU.mult, ["LR", "DTl"], ["RHO"])
        TT("dve", TH, LI, DTl, ALU.mult, ["LI", "DTl"], ["TH"])
        Dst = A.f32(128)
        Dst3 = Dst.rearrange("g (j h) -> g j h", j=8)
        for j in range(8):
            LD("sp", Dst3[0:32, j, :], dr["ssm_d"].rearrange("(g h) -> g h", h=16), ["Dst"], "c2")
        Dv = A.f32(G)
        TR(psf[1][:, 0:32], Dst[0:32, :], ident_f[0:32, 0:32], ["Dst", "ident_f"], ["psf1"])
        CP("dve", Dv, psf[1][:, 0:32], ["psf1"], ["Dv"])
        if stop == "0b1":
            return finish()

        PT = dict(PHI=A.f32(G * 25), K=A.i32(G * 25), Kf=A.f32(G * 25), R=A.f32(G * 25), S=A.f32(G * 25), C=A.f32(G * 25), M=A.f32(G * 25))

        def power_table(E, n, nm):
            sz = G * n
            PHI = PT["PHI"][:, 0:sz]; K = PT["K"][:, 0:sz]; Kf = PT["Kf"][:, 0:sz]; R = PT["R"][:, 0:sz]
            SINv = PT["S"][:, 0:sz]; COSv = PT["C"][:, 0:sz]; MAG = PT["M"][:, 0:sz]
            AR = A.f32(sz); AI = A.f32(sz)
            nm_e = "Ein" if nm == "i" else "Eout"
            nm = ""
            v3 = lambda t: t.rearrange("p (g n) -> p g n", g=G)
            TT("dve", v3(PHI), bc(TH, [128, G, n], 2), bc(E, [128, G, n], 1), ALU.mult, ["TH", nm_e], ["PHI" + nm])
            for which, dst in ((0, SINv), (1, COSv)):
                src = PHI
                if which == 1:
                    TS("dve", R, PHI, math.pi / 2, None, ALU.add, None, ["PHI" + nm], ["R" + nm])
                    src = R
                TS("dve", K, src, 1.0 / TWO_PI, None, ALU.mult, None, ["PHI" + nm, "R" + nm], ["K" + nm])
                CP("dve", Kf, K, ["K" + nm], ["Kf" + nm])
                STT(R, Kf, -TWO_PI, src, ALU.mult, ALU.add, ["Kf" + nm, "PHI" + nm, "R" + nm], ["R" + nm])
                TS("dve", R, R, 3.14159, -3.14159, ALU.min, ALU.max, ["R" + nm], ["R" + nm])
                ACT(dst, R, AF.Sin, ["R" + nm], ["sc%d" % which + nm])
            TT("dve", v3(MAG), bc(RHO, [128, G, n], 2), bc(E, [128, G, n], 1), ALU.mult, ["RHO", nm_e], ["MAG" + nm])
            ACT(MAG, MAG, AF.Exp, ["MAG" + nm], ["MAG" + nm])
            sfx = "i" if nm_e == "Ein" else "o"
            TT("dve", AR, MAG, COSv, ALU.mult, ["MAG" + nm, "sc1" + nm], ["AR" + sfx])
            TT("dve", AI, MAG, SINv, ALU.mult, ["MAG" + nm, "sc0" + nm], ["AI" + sfx])
            return v3(AR), v3(AI)

        ARi, AIi = power_table(Ein, 24, "i")
        ARo, AIo = power_table(Eout, 25, "o")

        if stop == "0b2":
            return finish()
        A1r = A.f32(G); A1i = A.f32(G)
        for (lo, hi, i1, i16) in ((0, 64, 1, 16), (64, 128, 16, 1)):
            CP("dve", A1r[lo:hi, :], ARo[lo:hi, :, i1], ["ARo"], ["A1r"])
            CP("dve", A1i[lo:hi, :], AIo[lo:hi, :, i1], ["AIo"], ["A1i"])
            CP("dve", AA1_3[lo:hi, 0, :], ARo[lo:hi, :, i16], ["ARo"], ["AA1"])
            CP("dve", AA1_3[lo:hi, 1, :], ARo[lo:hi, :, i16], ["ARo"], ["AA1"])
            CP("dve", AA2_3[lo:hi, 1, :], AIo[lo:hi, :, i16], ["AIo"], ["AA2"])
            TS("dve", AA2_3[lo:hi, 0, :], AIo[lo:hi, :, i16], -1.0, None, ALU.mult, None, ["AIo"], ["AA2"])
        pw_t = [A.f32(G), A.f32(G)]
        CP("dve", APW1[1], AA1_3, ["AA1"], ["APW"])
        CP("dve", APW2[1], AA2_3, ["AA2"], ["APW"])
        for m in range(2, 9):
            pr_, pi_ = APW1[m - 1][:, 0, :], APW2[m - 1][:, 1, :]
            ar_, ai_ = AA1_3[:, 0, :], AA2_3[:, 1, :]
            TT("dve", pw_t[0], pr_, ar_, ALU.mult, ["APW", "AA1"], ["pw0"])
            TT("dve", pw_t[1], pi_, ai_, ALU.mult, ["APW", "AA2"], ["pw1"])
            TT("dve", APW1[m][:, 0, :], pw_t[0], pw_t[1], ALU.subtract, ["pw0", "pw1"], ["APW"])
            CP("dve", APW1[m][:, 1, :], APW1[m][:, 0, :], ["APW"], ["APW"])
            TT("dve", pw_t[0], pr_, ai_, ALU.mult, ["APW", "AA2"], ["pw0"])
            TT("dve", pw_t[1], pi_, ar_, ALU.mult, ["APW", "AA1"], ["pw1"])
            TT("dve", APW2[m][:, 1, :], pw_t[0], pw_t[1], ALU.add, ["pw0", "pw1"], ["APW"])
            TS("dve", APW2[m][:, 0, :], APW2[m][:, 1, :], -1.0, None, ALU.mult, None, ["APW"], ["APW"])
        den = A.f32(G); nr = A.f32(G); fr = A.f32(G); fi = A.f32(G); t1 = A.f32(G); t2 = A.f32(G)
        TT("dve", den, LR, LR, ALU.mult, ["LR"], ["den"])
        TT("dve", t1, LI, LI, ALU.mult, ["LI"], ["t1"])
        TT("dve", den, den, t1, ALU.add, ["den", "t1"], ["den"])
        p.op("dve", lambda e: e.reciprocal(out=den, in_=den), ["den"], ["den"])
        TS("dve", nr, A1r, -1.0, None, ALU.add, None, ["A1r"], ["nr"])
        TT("dve", t1, nr, LR, ALU.mult, ["nr", "LR"], ["t1"])
        TT("dve", t2, A1i, LI, ALU.mult, ["A1i", "LI"], ["t2"])
        TT("dve", t1, t1, t2, ALU.add, ["t1", "t2"], ["t1"])
        TT("dve", fr, t1, den, ALU.mult, ["t1", "den"], ["fr"])
        TT("dve", t1, A1i, LR, ALU.mult, ["A1i", "LR"], ["t1"])
        TT("dve", t2, nr, LI, ALU.mult, ["nr", "LI"], ["t2"])
        TT("dve", t1, t1, t2, ALU.subtract, ["t1", "t2"], ["t1"])
        TT("dve", fi, t1, den, ALU.mult, ["t1", "den"], ["fi"])
        BR = A.f32(G * 16); BI = A.f32(G * 16); Bbr = A.f32(G * 16); Bbi = A.f32(G * 16); tB = A.f32(G * 16)
        g3 = lambda t: t.rearrange("p (g h) -> p g h", g=G)
        for nm, dst in (("b_re", BR), ("b_im", BI)):
            for d_ in range(2):
                for q in range(4):
                    LD("sp" if q % 2 == 0 else "act", g3(dst)[d_ * 64:(d_ + 1) * 64, q * 8:(q + 1) * 8, :],
                       dr[nm][d_, q * 8:(q + 1) * 8].rearrange("g p h -> p g h"), [nm], "c3")
        Fr3 = bc(fr, [128, G, 16], 2); Fi3 = bc(fi, [128, G, 16], 2)
        TT("dve", g3(Bbr), g3(BR), Fr3, ALU.mult, ["b_re", "fr"], ["Bbr"])
        TT("dve", g3(tB), g3(BI), Fi3, ALU.mult, ["b_im", "fi"], ["tB"])
        TT("dve", Bbr, Bbr, tB, ALU.subtract, ["Bbr", "tB"], ["Bbr"])
        TT("dve", g3(Bbi), g3(BI), Fr3, ALU.mult, ["b_im", "fr"], ["Bbi"])
        TT("dve", g3(tB), g3(BR), Fi3, ALU.mult, ["b_re", "fi"], ["tB"])
        TT("dve", Bbi, Bbi, tB, ALU.add, ["Bbi", "tB"], ["Bbi"])
        Cst = A.f32(4 * 2 * 64)
        Cst4 = Cst.rearrange("p (t d q) -> p t d q", t=4, d=2)
        CR = A.f32(G * 16); CI = A.f32(G * 16); nCR = A.f32(G * 16); nCI = A.f32(G * 16)
        Cst_b = A.f32(4 * 2 * 64)
        Cst4_b = Cst_b.rearrange("p (t d q) -> p t d q", t=4, d=2)
        for nm, dst, c4, ck in (("c_re", CR, Cst4, "CstA"), ("c_im", CI, Cst4_b, "CstB")):
            for d_ in range(2):
                LD("sp", c4[:, :, d_, :], dr[nm][d_].rearrange("(t g) h q -> (g h) t q", t=4), [ck], "cst" + nm, total=True)
            for t_ in range(4):
                TR(psf[2][:, t_ * 128:(t_ + 1) * 128], c4[:, t_, :, :].rearrange("p d q -> p (d q)"), ident_f, [ck, "ident_f"], ["psf2"])
            CP("dve", dst, psf[2][:, :], ["psf2"], [nm])
        TS("dve", nCR, CR, -1.0, None, ALU.mult, None, ["c_re"], ["nCR"])
        TS("dve", nCI, CI, -1.0, None, ALU.mult, None, ["c_im"], ["nCI"])

        if stop == "0b3":
            return finish()
        GQ = 4
        GB_R = A.bf(GQ * 24 * 16); GB_I = A.bf(GQ * 24 * 16)
        GC_R = A.bf(GQ * 25 * 16); GC_I = A.bf(GQ * 25 * 16)
        GB_R4 = GB_R.rearrange("p (g n h) -> p g n h", g=GQ, n=24); GB_I4 = GB_I.rearrange("p (g n h) -> p g n h", g=GQ, n=24)
        GC_R4 = GC_R.rearrange("p (g n h) -> p g n h", g=GQ, n=25); GC_I4 = GC_I.rearrange("p (g n h) -> p g n h", g=GQ, n=25)
        tmpA = [A.f32(GQ * 25 * 16), A.f32(GQ * 25 * 16)]
        tmpP = [A.f32(GQ * 25 * 16), A.f32(GQ * 25 * 16)]
        tz1 = A.f32(128); tz2 = A.f32(128)
        fl = lambda ap: ap.rearrange("p j h -> p (j h)")
        for q in range(G // GQ):
            gs = slice(q * GQ, (q + 1) * GQ)
            shp = [128, GQ, 24, 16]
            ta = [t[:, 0:GQ * 24 * 16].rearrange("p (g n h) -> p g n h", g=GQ, n=24) for t in tmpA]
            a_r = bc(ARi[:, gs, :], shp, 3); a_i = bc(AIi[:, gs, :], shp, 3)
            b_r = bc(g3(Bbr)[:, gs, :], shp, 2); b_i = bc(g3(Bbi)[:, gs, :], shp, 2)
            TT("dve", ta[0], a_r, b_r, ALU.mult, ["ARi", "Bbr"], ["tA0"])
            TT("dve", ta[1], a_i, b_i, ALU.mult, ["AIi", "Bbi"], ["tA1"])
            TT("dve", GB_R4, ta[0], ta[1], ALU.subtract, ["tA0", "tA1"], ["GB_R"])
            TT("dve", ta[0], a_i, b_r, ALU.mult, ["AIi", "Bbr"], ["tA0"])
            TT("dve", ta[1], a_r, b_i, ALU.mult, ["ARi", "Bbi"], ["tA1"])
            TT("dve", GB_I4, ta[0], ta[1], ALU.add, ["tA0", "tA1"], ["GB_I"])
            p.op("dve", lambda e: e.memset(GB_R4[64:128, :, 16:24, :], 0.0), [], ["GB_R"])
            p.op("dve", lambda e: e.memset(GB_I4[64:128, :, 16:24, :], 0.0), [], ["GB_I"])
            if stop == "g1":
                return finish()
            shp = [128, GQ, 25, 16]
            tp = [t.rearrange("p (g n h) -> p g n h", g=GQ, n=25) for t in tmpP]
            o_r = bc(ARo[:, gs, :], shp, 3); o_i = bc(AIo[:, gs, :], shp, 3)
            c_r = bc(g3(CR)[:, gs, :], shp, 2); c_i = bc(g3(CI)[:, gs, :], shp, 2)
            nc_r = bc(g3(nCR)[:, gs, :], shp, 2); nc_i = bc(g3(nCI)[:, gs, :], shp, 2)
            TT("dve", tp[0], o_r, c_r, ALU.mult, ["ARo", "c_re"], ["tP0"])
            TT("dve", tp[1], o_i, c_i, ALU.mult, ["AIo", "c_im"], ["tP1"])
            TT("dve", GC_R4, tp[0], tp[1], ALU.subtract, ["tP0", "tP1"], ["GC_R"])
            TT("dve", tp[0], o_i, nc_r, ALU.mult, ["AIo", "nCR"], ["tP0"])
            TT("dve", tp[1], o_r, nc_i, ALU.mult, ["ARo", "nCI"], ["tP1"])
            TT("dve", GC_I4, tp[0], tp[1], ALU.add, ["tP0", "tP1"], ["GC_I"])
            p.op("dve", lambda e: e.memset(GC_R4[0:64, :, 17:25, :], 0.0), [], ["GC_R"])
            p.op("dve", lambda e: e.memset(GC_I4[0:64, :, 17:25, :], 0.0), [], ["GC_I"])
            if stop == "g2":
                return finish()
            CP("act", W_oR4[:, gs], GC_R4[:, :, 1:17, :], ["GC_R"], ["W_oR"])
            CP("act", W_oI4[:, gs], GC_I4[:, :, 1:17, :], ["GC_I"], ["W_oI"])
            if stop == "g3":
                return finish()
            for g2 in range(0, GQ, 2):
                pb = psb[(g2 // 2) % 2]
                pk = "psb%d" % ((g2 // 2) % 2)
                k = 0
                for gl in (g2, g2 + 1):
                    for J in range(2):
                        for ri, GB4 in enumerate((GB_R4, GB_I4)):
                            TR(pb[:, k * 128:(k + 1) * 128], fl(GB4[:, gl, 8 * J:8 * J + 8, :]), ident_bf,
                               ["GB_R", "GB_I", "ident_bf"], [pk])
                            k += 1
                gg_ = q * GQ + g2
                CP("act", W_in[:, gg_ * 512:(gg_ + 2) * 512], pb[:, :], [pk], ["W_in"])
            if stop == "g4":
                dbg_list.extend([("ARi", ARi.rearrange("p g n -> p (g n)")), ("AIi", AIi.rearrange("p g n -> p (g n)")),
                                 ("ARo", ARo.rearrange("p g n -> p (g n)")), ("AIo", AIo.rearrange("p g n -> p (g n)")),
                                 ("Bbr", Bbr), ("Bbi", Bbi), ("CR", CR), ("CI", CI), ("Dv", Dv), ("fr", fr), ("fi", fi),
                                 ("LR", LR), ("LI", LI), ("DTl", DTl)])
                return finish()
            for gl in range(GQ):
                g = q * GQ + gl
                import os as _os
                _alt = 0 if _os.environ.get("NOALT") else (g % 2)
                psF = psf[2 + 2 * _alt]; pkF = "psf%d" % (2 + 2 * _alt)
                psB = psf[3 + 2 * _alt]; pkB = "psf%d" % (3 + 2 * _alt)
                specs = [
                    (psF, pkF, 0, (16, 24), (8, 16)),
                    (psB, pkB, 0, (16, 24), (0, 8)),
                    (psF, pkF, 1, (8, 16), (17, 25)),
                    (psB, pkB, 1, (0, 8), (17, 25)),
                ]
                for (ps_, pk, k, bi, ci) in specs:
                    MM(ps_[:, k * 128:(k + 1) * 128], fl(GB_R4[:, gl, bi[0]:bi[1], :]), fl(GC_R4[:, gl, ci[0]:ci[1], :]), True, False,
                       ["GB_R", "GC_R"], [pk])
                    MM(ps_[:, k * 128:(k + 1) * 128], fl(GB_I4[:, gl, bi[0]:bi[1], :]), fl(GC_I4[:, gl, ci[0]:ci[1], :]), False, True,
                       ["GB_I", "GC_I"], [pk])
                if stop == "g7" and gl == 1:
                    return finish()
                CP("act", Tz4[:, g, 0:2, :], psF[:, 0:256].rearrange("p (d m) -> p d m", d=2), [pkF], ["Tz%d" % g])
                if stop == "g8" and gl == 1:
                    CP("act", tmpP[0][:, 0:256], psF[:, 0:256], [pkF], ["dbgF"])
                    CP("act", tmpP[1][:, 0:256], psB[:, 0:256], [pkB], ["dbgB"])
                    dbg_list.extend([("psF", tmpP[0][:, 0:256]), ("psB", tmpP[1][:, 0:256]), ("Mge", Mge), ("Mle", Mle), ("tz1", tz1), ("tz2", tz2)])
                    return finish()
                TT("dve", tz1, psB[:, 0:128], Mge, ALU.mult, [pkB, "Mge"], ["tz1"])
                TT("dve", tz2, psB[:, 128:256], Mle, ALU.mult, [pkB, "Mle"], ["tz2"])
                TT("dve", tz1, tz1, tz2, ALU.add, ["tz1", "tz2"], ["tz1"])
                TS("dve", tz2, ident_f, Dv[:, g:g + 1], None, ALU.mult, None, ["ident_f", "Dv", "tz2"], ["tz2"])
                TT("dve", Tz4[:, g, 2, :], tz1, tz2, ALU.add, ["tz1", "tz2"], ["Tz%dm" % g])
                if stop == "g5":
                    return finish()
                if stop == "g9" and gl == 1:
                    return finish()
                if stop == "g10" and gl == 2:
                    return finish()
            if stop == "g6":
                return finish()
        p.barrier()
        if stop == "0b":
            return finish()
        A.top = persist_top

        w_ua = A.bf(8 * 512)
        w_ua3 = w_ua.rearrange("p (k n) -> p k n", k=8)
        bua_rep = A.f32(512)
        LD("pool", w_ua3, dr["w_in"][:, 0:512].rearrange("(k p) n -> p k n", p=128), ["w_ua"], "c4")
        LD("sp", bua_rep, dr["b_in"][0:512].partition_broadcast(128), ["bua"], "c5")
        V = A.bf(G * 256); V3 = V.rearrange("p (g s) -> p g s", g=G)
        m0 = A.top
        hT = A.bf(8 * 1024); hT3 = hT.rearrange("p (k n) -> p k n", k=8)
        Z = A.bf(G * 128); Z4 = Z.rearrange("p (g j h) -> p g j h", g=G, j=8)
        NXS = 6
        xs = [A.f32(1024) for _ in range(NXS)]
        xns = [A.bf(1024), A.bf(1024)]
        stats = [A.f32(16) for _ in range(NXS)]
        topX = A.top
        A.top = m0
        LL = A.f32(2 * G * 128); LL4 = LL.rearrange("p (r g c) -> p r g c", r=2, g=G)
        LLn = LL.rearrange("p (r j g k) -> p r j g k", r=2, g=G, j=8)
        Sin = A.bf(2 * G * 128); Sin4 = Sin.rearrange("p (r g c) -> p r g c", r=2, g=G)
        Yz = A.bf(8 * 128)
        Zo = A.bf(2 * 8 * 512); Zo5 = Zo.rearrange("p (i t g h) -> p i t g h", i=2, t=8, g=G)
        Gc = A.f32(2 * G * 17); Gc4 = Gc.rearrange("p (r g k) -> p r g k", r=2, g=G)
        ZoF = Zo.bitcast(F32)
        WTW = ZoF[:, 0:1024]
        Ts2 = [ZoF[:, 2048:2112], ZoF[:, 2112:2176]]
        topY = A.top
        A.top = m0
        Lc = A.f32(2 * G * 64); Lc5 = Lc.rearrange("p (r g b c) -> p r g b c", r=2, g=G, b=BPC)
        Sc = A.f32(2 * G * BPC); Sc4 = Sc.rearrange("p (r g b) -> p r g b", r=2, g=G)
        Tc1 = [A.f32(2 * G * BPC), A.f32(2 * G * BPC)]; Tc2 = [A.f32(2 * G * BPC), A.f32(2 * G * BPC)]
        A.top = max(topX, topY, A.top)
        stageA_top = A.top
        ucnt = [0]
        VKEYS = ["V%d_%d" % (h_, g_) for h_ in range(2) for g_ in range(4)]

        def ln_tile(src_ap, slot, modcol, hT_dst, width, tagp):
            i = ucnt[0]; ucnt[0] += 1
            slot = i % NXS
            xk = "xs%d" % slot
            stat = stats[slot]; sk = "stat%d" % slot
            xn = xns[i % 2]; xnk = "xn%d" % (i % 2)
            LD("sp" if i % 2 == 0 else "act", xs[slot], src_ap, [xk], "x%d" % slot, total=False)
            p.op("dve", lambda e: e.bn_stats(out=stat[:, 0:6], in_=xs[slot][:, 0:512]), [xk], [sk], dur=700.0)
            p.op("dve", lambda e: e.bn_stats(out=stat[:, 6:12], in_=xs[slot][:, 512:1024]), [xk], [sk], dur=700.0)
            p.op("dve", lambda e: e.bn_aggr(out=stat[:, 12:14], in_=stat[:, 0:12]), [sk], [sk + "mv"], dur=250.0)
            TS("pool", stat[:, 14:15], stat[:, 13:14], EPS, None, ALU.add, None, [sk + "mv"], [sk + "rs"])
            TT("pool", stat[:, 14:15], stat[:, 14:15], mhalf[:, 0:1], ALU.pow, [sk + "rs", "mhalf"], [sk + "rs"])
            TS("dve", xn, xs[slot], stat[:, 12:13], stat[:, 14:15], ALU.subtract, ALU.mult, [xk, sk + "mv", sk + "rs"], [xnk])
            pb = psb[i % 2]; pk = "psb%d" % (i % 2)
            for kt in range(8):
                TR(pb[:, kt * 128:(kt + 1) * 128], xn[:, kt * 128:(kt + 1) * 128], ident_bf, [xnk, "ident_bf"], [pk])
            for kt in range(8):
                if i % 2 == 0:
                    ACT(hT_dst(kt), pb[:, kt * 128:(kt + 1) * 128], AF.Identity, [pk, "ST"], [tagp],
                        scale=ST3[:, 8 + kt, modcol:modcol + 1], bias=ST3[:, kt, modcol:modcol + 1])
                else:
                    TS("dve", hT_dst(kt), pb[:, kt * 128:(kt + 1) * 128], ST3[:, 8 + kt, modcol:modcol + 1],
                       ST3[:, kt, modcol:modcol + 1], ALU.mult, ALU.add, [pk, "ST"], [tagp])

        def tile1024(srcs, modcols, Vdst, vtag="V"):
            for s in range(8):
                ln_tile(srcs[s], s % 2, modcols[s], lambda kt, s=s: hT3[:, kt, s * 128:(s + 1) * 128], 128, "hT%d" % s)
            hT4 = hT.rearrange("p (k c j) -> p k j c", k=8, j=8)
            for j in range(8):
                ps_ = psf[j % 4]; pk = "psf%d" % (j % 4)
                for kt in range(8):
                    MM(ps_[:, :], hT4[:, kt, j, :], w_ua3[:, kt, :], kt == 0, kt == 7, ["hT%d" % s_ for s_ in range(8)] + ["w_ua"], [pk])
                TT("dve", Z4[:, :, j, :], ps_[:, :].rearrange("p (g h) -> p g h", g=G), bua_rep.rearrange("p (g h) -> p g h", g=G),
                   ALU.add, [pk, "bua"], ["Z%d" % j])
            for g8 in range(4):
                pb = psb[g8 % 2]; pk = "psb%d" % (g8 % 2)
                for k in range(8):
                    g = g8 * 8 + k
                    TR(pb[:, k * 128:(k + 1) * 128], Z4[:, g, :, :].rearrange("p j h -> p (j h)"), ident_bf, ["Z%d" % j_ for j_ in range(8)] + ["ident_bf"], [pk])
                CP("act" if g8 % 2 == 0 else "dve", Vdst[:, g8 * 8:(g8 + 1) * 8, :], pb[:, :].rearrange("p (g s) -> p g s", g=8), [pk], ["%s_%d" % (vtag, g8)])

        srcs = [dr["ctx"][s * 128:(s + 1) * 128, :] for s in range(8)]
        Vc = V3[:, :, 0:128]
        tile1024(srcs, [4] * 8, Vc, "V0")
        p.barrier()
        Vc4 = V.rearrange("p (g s) -> p g s", g=G)[:, :, 0:128].rearrange("p g (c j) -> p g j c", j=2)
        for g4 in range(0, G, 4):
            ps_ = psf[(g4 // 4) % 4]; pk = "psf%d" % ((g4 // 4) % 4)
            for k in range(4):
                g = g4 + k
                for ri in range(2):
                    col = (k * 2 + ri) * 64
                    for J in range(2):
                        MM(ps_[:, col:col + 64], W_in5[:, g, J, ri, :], Vc4[:, g, J, :], J == 0, J == 1, ["W_in"] + VKEYS, [pk])
            CP("dve", Lc5[:, :, g4:g4 + 4, :, :].rearrange("p r g b c -> p g r (b c)"),
               ps_[:, :].rearrange("p (g r n) -> p g r n", g=4, r=2), [pk], ["Lc"])
        for (lo, hi, eng, order) in ((0, 64, "dve", list(range(16))), (64, 128, "pool", list(range(15, -1, -1)))):
            tg = "c%d" % lo
            a1 = bc(AA1_3[lo:hi], [hi - lo, 2, G, BPC], 3); a2 = bc(AA2_3[lo:hi], [hi - lo, 2, G, BPC], 3)
            S = Sc4[lo:hi]
            t1_ = Tc1[lo // 64][lo:hi].rearrange("p (r g b) -> p r g b", r=2, g=G)
            t2_ = Tc2[lo // 64][lo:hi].rearrange("p (r g b) -> p r g b", r=2, g=G)
            for n, c in enumerate(order):
                Lcur = Lc5[lo:hi, :, :, :, c]
                if n == 0:
                    CP(eng, S, Lcur, ["Lc"], ["S" + tg])
                    continue
                TT(eng, t1_, S, a1, ALU.mult, ["S" + tg, "AA1"], ["t1" + tg])
                TT(eng, t2_[:, 0], S[:, 1], a2[:, 0], ALU.mult, ["S" + tg, "AA2"], ["t2" + tg])
                TT(eng, t2_[:, 1], S[:, 0], a2[:, 1], ALU.mult, ["S" + tg, "AA2"], ["t2" + tg])
                TT(eng, t1_, t1_, t2_, ALU.add, ["t1" + tg, "t2" + tg], ["t1" + tg])
                TT(eng, S, t1_, Lcur, ALU.add, ["t1" + tg, "Lc"], ["S" + tg])
            CP(eng, H04[lo:hi], S, ["S" + tg], ["H0"])
        p.barrier()
        if stop == "C":
            return finish()

        for b in range(BPC):
            for half in range(2):
                srcs = [dr["x"][b, half * 1024 + s * 128: half * 1024 + (s + 1) * 128, :] for s in range(8)]
                tile1024(srcs, [b] * 8, V3[:, :, half * 128:(half + 1) * 128], "V%d" % half)
            p.barrier()
            Vj = V3.rearrange("p g (c j) -> p g j c", j=2)
            for g2 in range(0, G, 2):
                ps_ = psf[(g2 // 2) % 4]; pk = "psf%d" % ((g2 // 2) % 4)
                for k in range(2):
                    g = g2 + k
                    for ri in range(2):
                        col = (k * 2 + ri) * 128
                        for J in range(2):
                            MM(ps_[0:64, col:col + 128], W_in5[:, g, J, ri, 0:64], Vj[:, g, J, :], J == 0, J == 1, ["W_in"] + VKEYS, [pk])
                        for J in range(2):
                            MM(ps_[64:128, col:col + 128], W_in5[:, g, J, ri, 64:128], Vj[:, g, J, ::-1], J == 0, J == 1, ["W_in"] + VKEYS, [pk])
                for k in range(2):
                    CP("act" if (g2 // 2) % 2 == 0 else "dve", LLn[:, :, :, g2 + k, :],
                       ps_[:, k * 256:(k + 1) * 256].rearrange("p (r k j) -> p r j k", r=2, j=8), [pk], ["LL%d" % (g2 + k)])
            GS = 10
            LLKEYS = ["LL%d" % g_ for g_ in range(G)]
            chains = [(0, 128, True, "dve", "dve", 0, G, WTW)]
            for (lo, hi, fwd, eng, eng2, g0, g1, wt) in chains:
                tg = "s%d_%d" % (lo, g0)
                ng = g1 - g0
                key = "LL" + tg
                LLv = LLn[lo:hi, :, :, g0:g1, :].rearrange("p r j g k -> p r g k j")
                Sv = Sin4[lo:hi, :, g0:g1, :].rearrange("p r g (k j) -> p r g k j", j=8)
                Gv = Gc4[lo:hi, :, g0:g1, :]
                tw = wt[lo:hi, 0:2 * ng * 16].rearrange("p (r g k) -> p r g k", r=2, g=ng)

                def cmad(dst, src, m, t, shp, rkeys, wkeys, bcast, eng=eng):
                    a1 = APW1[m][lo:hi, :, g0:g1]; a2 = APW2[m][lo:hi, :, g0:g1]
                    if bcast:
                        a1 = bc(a1, shp, 3); a2 = bc(a2, shp, 3)
                    tk = "tw" + tg + ("" if bcast else "s")
                    TT(eng, t, src, a1, ALU.mult, rkeys + ["APW"], [tk])
                    TT(eng, dst, dst, t, ALU.add, [tk] + rkeys + wkeys, wkeys)
                    TT(eng, t[:, 0], src[:, 1], a2[:, 0], ALU.mult, rkeys + ["APW"], [tk])
                    TT(eng, t[:, 1], src[:, 0], a2[:, 1], ALU.mult, rkeys + ["APW"], [tk])
                    TT(eng, dst, dst, t, ALU.add, [tk] + rkeys + wkeys, wkeys)

                shp = [hi - lo, 2, ng, 16]
                js = list(range(1, 8)) if fwd else list(range(6, -1, -1))
                for j in js:
                    jp = j - 1 if fwd else j + 1
                    cmad(LLv[:, :, :, :, j], LLv[:, :, :, :, jp], 1, tw, shp, LLKEYS[g0:g1] + [key], [key], True)
                ts_ = Ts2[lo // 64][lo:hi, 0:2 * ng].rearrange("p (r g) -> p r g", r=2)
                if fwd:
                    CP(eng2, Gv[:, :, :, 0], H04[lo:hi, :, g0:g1, b], ["H0"], ["Gc" + tg])
                    CP(eng2, Gv[:, :, :, 1:17], LLv[:, :, :, :, 7], [key], ["Gc" + tg])
                    for k in range(16):
                        cmad(Gv[:, :, :, k + 1], Gv[:, :, :, k], 8, ts_, None, ["Gc" + tg], ["Gc" + tg], False, eng=eng2)
                else:
                    CP(eng2, Gv[:, :, :, 16], H04[lo:hi, :, g0:g1, b], ["H0"], ["Gc" + tg])
                    CP(eng2, Gv[:, :, :, 0:16], LLv[:, :, :, :, 0], [key], ["Gc" + tg])
                    for k in range(15, -1, -1):
                        cmad(Gv[:, :, :, k], Gv[:, :, :, k + 1], 8, ts_, None, ["Gc" + tg], ["Gc" + tg], False, eng=eng2)
                Gin = Gv[:, :, :, 0:16] if fwd else Gv[:, :, :, 1:17]
                skey = "Sin" + ("a0" if fwd else "a64")
                for j in range(8):
                    m = j if fwd else 7 - j
                    SvN = Sin4[:, :, g0:g1, :].rearrange("p r g (k j) -> p r g k j", j=8)
                    if m == 0:
                        CP(eng, SvN[0:64, :, :, :, j], Gin[0:64], ["Gc" + tg], [skey + tg])
                        CP(eng, SvN[64:128, :, :, ::-1, 7 - j], Gin[64:128], ["Gc" + tg], [skey + tg + "b"])
                        continue
                    jp = j - 1 if fwd else j + 1
                    Pj = LLv[:, :, :, :, jp]
                    a1 = bc(APW1[m][lo:hi, :, g0:g1], shp, 3); a2 = bc(APW2[m][lo:hi, :, g0:g1], shp, 3)
                    TT(eng, tw, Gin, a1, ALU.mult, ["Gc" + tg, "APW"], ["tw" + tg])
                    TT(eng, Pj, Pj, tw, ALU.add, ["tw" + tg, key], [key])
                    TT(eng, tw[:, 0], Gin[:, 1], a2[:, 0], ALU.mult, ["Gc" + tg, "APW"], ["tw" + tg])
                    TT(eng, tw[:, 1], Gin[:, 0], a2[:, 1], ALU.mult, ["Gc" + tg, "APW"], ["tw" + tg])
                    TT(eng, SvN[0:64, :, :, :, j], Pj[0:64], tw[0:64], ALU.add, ["tw" + tg, key], [skey + tg])
                    TT(eng, SvN[64:128, :, :, ::-1, 7 - j], Pj[64:128], tw[64:128], ALU.add, ["tw" + tg, key], [skey + tg + "b"])
            DELTA = {(0, 0): 2, (0, 1): 1, (1, 0): 0, (1, 1): 2}
            SINKEYS = ["Sina0s0_0", "Sina0s0_0b"]
            for g4 in range(0, G, 2):
                ps_ = psf[(g4 // 2) % 4]; pk = "psf%d" % ((g4 // 2) % 4)
                for k in range(2):
                    g = g4 + k
                    for I in range(2):
                        col = (k * 2 + I) * 128
                        o = ps_[:, col:col + 128]
                        MM(o, Tz4[:, g, DELTA[(I, 0)], :], Vj[:, g, 0, :], True, False, ["Tz"] + VKEYS, [pk])
                        MM(o, Tz4[:, g, DELTA[(I, 1)], :], Vj[:, g, 1, :], False, False, ["Tz"] + VKEYS, [pk])
                        MM(o, W_oR4[:, g, 8 * I:8 * I + 8, :].rearrange("p t h -> p (t h)"), Sin4[:, 0, g, :], False, False,
                           ["W_oR"] + SINKEYS, [pk])
                        MM(o, W_oI4[:, g, 8 * I:8 * I + 8, :].rearrange("p t h -> p (t h)"), Sin4[:, 1, g, :], False, True,
                           ["W_oI"] + SINKEYS, [pk])
                yk = "Yz%d" % ((g4 // 2) % 2)
                yz = Yz[:, ((g4 // 2) % 2) * 512:((g4 // 2) % 2 + 1) * 512]
                CP("act" if (g4 // 2) % 2 == 0 else "dve", yz, ps_[:, :], [pk], [yk])
                pb = psb[(g4 // 2) % 2]; pbk = "psb%d" % ((g4 // 2) % 2)
                for k in range(2):
                    for I in range(2):
                        q = k * 2 + I
                        TR(pb[:, q * 128:(q + 1) * 128], yz[:, q * 128:(q + 1) * 128], ident_bf, [yk, "ident_bf"], [pbk])
                for k in range(2):
                    CP("dve" if (g4 // 2) % 2 == 0 else "act", Zo5[:, :, :, g4 + k, :],
                       pb[:, k * 256:(k + 1) * 256].rearrange("p (i t h) -> p i t h", i=2, t=8), [pbk], ["Zo%d" % (g4 + k)])
            p.dma("sp", yscr[b].rearrange("(c it) ch -> c (it ch)", it=16), Zo, reads=["Zo%d" % g_ for g_ in range(G)], writes=["yscr"], stream="ys", total=False)
            p.barrier()
            if stop == "A1":
                return finish()
        if stop == "A":
            return finish()

        A.top = small_top
        NS = TB // 128
        NT = BPC * (L // TB)
        w_in2 = A.bf(8 * 4096); w_in23 = w_in2.rearrange("p (k n) -> p k n", k=8)
        w_glu = A.bf(4 * 512); w_glu3 = w_glu.rearrange("p (k n) -> p k n", k=4)
        w_a = A.bf(4 * 1024); w_a3 = w_a.rearrange("p (k n) -> p k n", k=4)
        w_b = A.bf(4 * 1024); w_b3 = w_b.rearrange("p (k n) -> p k n", k=4)
        w_o = A.bf(8 * 1024); w_o3 = w_o.rearrange("p (k n) -> p k n", k=8)
        wsT = A.bf(8 * 128); wsT3 = wsT.rearrange("p (g n) -> p g n", g=8)
        bcol = A.f32(32)
        gcol = A.f32(4)
        bs_t = A.f32(4 * 128); bs_t3 = bs_t.rearrange("p (q n) -> p q n", q=4)
        bvb_rep = A.f32(512); sg_rep = A.f32(512); sb_rep = A.f32(512)
        lng_rep = A.f32(1024); lnb_rep = A.f32(1024)
        gate_rep = [A.f32(1024), A.f32(1024)]
        ones_r = A.bf(128); bout_r = A.bf(1024)
        xn = A.bf(1024)
        hB = [A.bf(8 * TB), A.bf(8 * TB)]
        hB3 = [h.rearrange("p (k n) -> p k n", k=8) for h in hB]
        xB = [[A.f32(1024) for _ in range(NS)] for _ in range(2)]
        bufs = {}
        for nm in ("za", "ub", "zb", "gT", "gg", "yb"):
            bufs[nm] = A.bf(4 * TB).rearrange("p (k n) -> p k n", k=4)
        sga = A.bf(8 * TB).rearrange("p (k n) -> p k n", k=8)
        sgb = A.bf(8 * TB).rearrange("p (k n) -> p k n", k=8)
        mB = A.bf(8 * TB); mB3 = mB.rearrange("p (k n) -> p k n", k=8)
        vtm = A.bf(NS * 512); vtm3 = vtm.rearrange("p (s n) -> p s n", s=NS)
        ytm = A.bf(NS * 512); ytm3 = ytm.rearrange("p (s n) -> p s n", s=NS)
        vf = [A.f32(512), A.f32(512)]
        mt = [vf[0][:, 0:TB], vf[1][:, 0:TB]]
        rr = [A.f32(1024), A.f32(1024)]; ro = rr
        statX = A.f32(32); statV = A.f32(32); statO = A.f32(32)
        ws_st = rr[0].bitcast(BF16)[:, 0:1024]; ws_st3 = ws_st.rearrange("p (g m) -> p g m", g=8)
        for q in range(4):
            LD("pool", w_in23[:, 2 * q:2 * q + 2, :], dr["w_in"][q * 256:(q + 1) * 256, 512:DIN].rearrange("(k p) n -> p k n", p=128), ["w_in2"], "w0")
        LD("pool", w_glu3, dr["glu_w"].rearrange("(k p) n -> p k n", p=128), ["w_glu"], "w0")
        LD("pool", w_a3, dr["w_a"].rearrange("(k p) n -> p k n", p=128), ["w_a"], "w0")
        LD("pool", w_b3, dr["w_b"].rearrange("(k p) n -> p k n", p=128), ["w_b"], "w0")
        LD("pool", w_o3, dr["w_out"].rearrange("(k p) n -> p k n", p=128), ["w_o"], "w0")
        LD("pool", ws_st3, dr["sgu_w"].rearrange("g n m -> n g m"), ["ws_st"], "w0")
        LD("pool", bout_r[0:1, :], dr["b_out"].rearrange("(o n) -> o n", o=1), ["bout_r"], "w0")
        p.op("dve", lambda e: e.memset(ones_r[0:1, :], 1.0), [], ["ones_r"])
        LD("sp", bcol, dr["b_in"][512:DIN].rearrange("(t p) -> p t", p=128), ["bcol"], "w1", allow_slow_non_contiguous=True)
        LD("sp", gcol, dr["glu_b"].rearrange("(t p) -> p t", p=128), ["gcol"], "w1", allow_slow_non_contiguous=True)
        for q in range(4):
            for hh in range(2):
                LD("sp", bs_t3[hh * 64:(hh + 1) * 64, q, :], dr["sgu_b"][2 * q + hh].partition_broadcast(64), ["bs_t"], "w1")
        LD("sp", bvb_rep, dr["b_in"][1536:2048].partition_broadcast(128), ["bvb"], "w1")
        LD("sp", sg_rep, dr["sgu_ln_g"].partition_broadcast(128), ["sg"], "w1")
        LD("sp", sb_rep, dr["sgu_ln_b"].partition_broadcast(128), ["sb"], "w1")
        LD("sp", lng_rep, dr["ln_g"].partition_broadcast(128), ["lng"], "w1")
        LD("sp", lnb_rep, dr["ln_b"].partition_broadcast(128), ["lnb"], "w1")
        for g in range(8):
            TR(psb[0][:, g * 128:(g + 1) * 128], ws_st3[:, g, :], ident_bf, ["ws_st", "ident_bf"], ["psb0"])
        CP("dve", wsT, psb[0][:, :], ["psb0"], ["wsT"])
        p.barrier()

        def rstd_chain(stat, n, tag):
            TS("pool", stat[:, 20:20 + n], stat[:, 13:13 + 2 * n].rearrange("p (j t) -> p j t", t=2)[:, :, 0], EPS, None, ALU.add, None,
               ["mv" + tag], ["rs" + tag])
            TT("pool", stat[:, 20:20 + n], stat[:, 20:20 + n], mhalf[:, 0:n], ALU.pow, ["rs" + tag, "mhalf"], ["rs" + tag])

        def F0(i):
            b = i // (L // TB); tok0 = (i % (L // TB)) * TB
            par = i % 2
            for s in range(NS):
                xk = "xB%d_%d" % (par, s)
                LD("sp" if s % 2 == 0 else "act", xB[par][s], dr["x"][b, tok0 + s * 128:tok0 + (s + 1) * 128, :], [xk], "xb%d_%d" % (par, s), total=False)
                p.op("dve", lambda e, s=s: e.bn_stats(out=statX[:, 0:6], in_=xB[par][s][:, 0:512]), [xk], ["stX"])
                p.op("dve", lambda e, s=s: e.bn_stats(out=statX[:, 6:12], in_=xB[par][s][:, 512:1024]), [xk], ["stX"])
                p.op("dve", lambda e, s=s: e.bn_aggr(out=statX[:, 12 + 2 * s:14 + 2 * s], in_=statX[:, 0:12]), ["stX"], ["mvX"])
            rstd_chain(statX, NS, "X")
            for s in range(NS):
                xk = "xB%d_%d" % (par, s)
                TS("dve", xn, xB[par][s], statX[:, 12 + 2 * s:13 + 2 * s], statX[:, 20 + s:21 + s], ALU.subtract, ALU.mult, [xk, "mvX", "rsX"], ["xn"])
                pb = psb[s % 2]; pk = "psb%d" % (s % 2)
                for kt in range(8):
                    TR(pb[:, kt * 128:(kt + 1) * 128], xn[:, kt * 128:(kt + 1) * 128], ident_bf, ["xn", "ident_bf"], [pk])
                for kt in range(8):
                    dst = hB3[par][:, kt, s * 128:(s + 1) * 128]
                    if s % 2 == 0:
                        ACT(dst, pb[:, kt * 128:(kt + 1) * 128], AF.Identity, [pk, "ST"], ["hB%d_%d" % (par, s)],
                            scale=ST3[:, 8 + kt, b:b + 1], bias=ST3[:, kt, b:b + 1])
                    else:
                        TS("dve", dst, pb[:, kt * 128:(kt + 1) * 128], ST3[:, 8 + kt, b:b + 1], ST3[:, kt, b:b + 1],
                           ALU.mult, ALU.add, [pk, "ST"], ["hB%d_%d" % (par, s)])

        pcnt = [0]

        def proj_fm(h3, hk, col0, ntile, func, dst3, key):
            for t_ in range(ntile):
                q = pcnt[0] % 4; pcnt[0] += 1
                ps_ = psf[q]; pk = "psf%d" % q
                c0 = col0 + t_ * 128
                for kt in range(8):
                    MM(ps_[:, 0:TB], w_in23[:, kt, c0:c0 + 128], h3[:, kt, :], kt == 0, kt == 7, ["w_in2"] + hk, [pk])
                ACT(dst3[:, t_, :], ps_[:, 0:TB], func, [pk, "bcol"], [key], bias=bcol[:, c0 // 128:c0 // 128 + 1])

        oc = [0]

        def MAIN(i):
            b = i // (L // TB); tok0 = (i % (L // TB)) * TB
            par = i % 2
            h3 = hB3[par]; hk = ["hB%d_%d" % (par, s_) for s_ in range(NS)]
            gr = gate_rep[b % 2]; grk = "gate_rep%d" % (b % 2)
            if i % (L // TB) == 0:
                for hh in range(2):
                    MM(psf[4][:, :], sel[0:5, b * 128:(b + 1) * 128], grow[0:5, hh * 512:(hh + 1) * 512], True, True, ["sel", "grow"], ["psf4"])
                    CP("dve", gr[:, hh * 512:(hh + 1) * 512], psf[4][:, :], ["psf4"], [grk])
            for s in range(NS):
                LD("sp", ytm3[:, s, :], yscr[b, tok0 + s * 128:tok0 + (s + 1) * 128, :], ["ytm%d" % s], "yl%d" % s, total=False)
                pb = psb[s % 2]; pk = "psb%d" % (s % 2)
                for q in range(4):
                    TR(pb[:, q * 128:(q + 1) * 128], ytm3[:, s, q * 128:(q + 1) * 128], ident_bf, ["ytm%d" % s, "ident_bf"], [pk])
                ACT(bufs["gT"][:, :, s * 128:(s + 1) * 128], pb[:, 0:512].rearrange("p (q n) -> p q n", q=4), AF.Gelu_apprx_tanh, [pk], ["gT"])
            proj_fm(h3, hk, 512, 4, AF.Gelu_apprx_tanh, bufs["ub"], "ub")
            for s in range(NS):
                ps_ = psf[4 + s % 2]; pk = "psf%d" % (4 + s % 2)
                for kt in range(8):
                    MM(ps_[:, :], h3[:, kt, s * 128:(s + 1) * 128], w_in23[:, kt, 1024:1536], kt == 0, kt == 7, [hk[s], "w_in2"], [pk])
                TT("dve", vf[s], ps_[:, :], bvb_rep, ALU.add, [pk, "bvb"], ["vf%d" % s])
                ACT(vf[s], vf[s], AF.Gelu_apprx_tanh, ["vf%d" % s], ["vf%d" % s])
                p.op("dve", lambda e, s=s: e.bn_stats(out=statV[:, 0:6], in_=vf[s]), ["vf%d" % s], ["stV"])
                p.op("dve", lambda e, s=s: e.bn_aggr(out=statV[:, 12 + 2 * s:14 + 2 * s], in_=statV[:, 0:6]), ["stV"], ["mvV"])
            proj_fm(h3, hk, 1536, 4, AF.Silu, bufs["zb"], "zb")
            proj_fm(h3, hk, 0, 4, AF.Silu, bufs["za"], "za")
            rstd_chain(statV, NS, "V")
            for s in range(NS):
                TS("dve", vf[s], vf[s], statV[:, 12 + 2 * s:13 + 2 * s], statV[:, 20 + s:21 + s], ALU.subtract, ALU.mult,
                   ["vf%d" % s, "mvV", "rsV"], ["vf%d" % s])
                TT("dve", vf[s], vf[s], sg_rep, ALU.mult, ["vf%d" % s, "sg"], ["vf%d" % s])
                TT("dve", vtm3[:, s, :], vf[s], sb_rep, ALU.add, ["vf%d" % s, "sb"], ["vtm"])
            proj_fm(h3, hk, 3072, 8, AF.Sigmoid, sgb, "sgb")
            proj_fm(h3, hk, 2048, 8, AF.Sigmoid, sga, "sga")
            for t_ in range(4):
                q = pcnt[0] % 4; pcnt[0] += 1
                ps_ = psf[q]; pk = "psf%d" % q
                for kt in range(4):
                    MM(ps_[:, 0:TB], w_glu3[:, kt, t_ * 128:(t_ + 1) * 128], bufs["gT"][:, kt, :], kt == 0, kt == 3, ["w_glu", "gT"], [pk])
                ACT(bufs["gg"][:, t_, :], ps_[:, 0:TB], AF.Sigmoid, [pk, "gcol"], ["gg"], bias=gcol[:, t_:t_ + 1])
            TT("pool", bufs["ub"], bufs["ub"], bufs["zb"], ALU.mult, ["ub", "zb"], ["ub"])
            for s in range(NS):
                for q in range(4):
                    qq = pcnt[0] % 4; pcnt[0] += 1
                    pq = psf[qq]; pqk = "psf%d" % qq
                    for hh in range(2):
                        g = 2 * q + hh
                        MM(pq[hh * 64:(hh + 1) * 64, 0:128], vtm3[:, s, g * 64:(g + 1) * 64], wsT3[:, g, :], True, True, ["vtm", "wsT"], [pqk])
                    TT("dve", bufs["yb"][:, q, s * 128:(s + 1) * 128], pq[:, 0:128], bs_t3[:, q, :], ALU.add, [pqk, "bs_t"], ["yb"])
            TT("dve", bufs["yb"], bufs["yb"], bufs["ub"], ALU.mult, ["yb", "ub"], ["yb"])
            TT("pool", bufs["gg"], bufs["gg"], bufs["gT"], ALU.mult, ["gg", "gT"], ["gg"])
            for t_ in range(8):
                q = pcnt[0] % 4; pcnt[0] += 1
                ps_ = psf[q]; pk = "psf%d" % q
                for kt in range(4):
                    MM(ps_[:, 0:TB], w_b3[:, kt, t_ * 128:(t_ + 1) * 128], bufs["yb"][:, kt, :], kt == 0, kt == 3, ["w_b", "yb"], [pk])
                TT("dve", mB3[:, t_, :], ps_[:, 0:TB], sgb[:, t_, :], ALU.mult, [pk, "sgb"], ["mB%d" % t_])
            TT("dve", bufs["gg"], bufs["gg"], bufs["za"], ALU.mult, ["gg", "za"], ["gg"])
            for t_ in range(8):
                q = pcnt[0] % 4; pcnt[0] += 1
                ps_ = psf[q]; pk = "psf%d" % q
                for kt in range(4):
                    MM(ps_[:, 0:TB], w_a3[:, kt, t_ * 128:(t_ + 1) * 128], bufs["gg"][:, kt, :], kt == 0, kt == 3, ["w_a", "gg"], [pk])
                TT("dve", mt[t_ % 2], ps_[:, 0:TB], sga[:, t_, :], ALU.mult, [pk, "sga"], ["vf%d" % (t_ % 2)])
                TT("pool" if t_ % 2 == 0 else "dve", mB3[:, t_, :], mB3[:, t_, :], mt[t_ % 2], ALU.add, ["vf%d" % (t_ % 2), "mB%d" % t_], ["mB%d" % t_])
            for s in range(NS):
                o_i = oc[0] % 2; oc[0] += 1
                rk = "rr%d" % o_i; ok = rk
                for hh in range(2):
                    ps_ = psf[4 + hh]; pk = "psf%d" % (4 + hh)
                    MM(ps_[:, :], ones_r[0:1, :], bout_r[0:1, hh * 512:(hh + 1) * 512], True, False, ["ones_r", "bout_r"], [pk])
                    for kt in range(8):
                        MM(ps_[:, :], mB3[:, kt, s * 128:(s + 1) * 128], w_o3[:, kt, hh * 512:(hh + 1) * 512], False, kt == 7, ["mB%d" % kt, "w_o"], [pk])
                    TT("dve", rr[o_i][:, hh * 512:(hh + 1) * 512], ps_[:, :], gr[:, hh * 512:(hh + 1) * 512], ALU.mult, [pk, grk], [rk, rk + "a", rk + "b"])
                STT(rr[o_i], xB[par][s], ALPHA, rr[o_i], ALU.mult, ALU.add, ["xB%d_%d" % (par, s), rk], [rk])
                p.op("dve", lambda e, o_i=o_i: e.bn_stats(out=statO[:, 0:6], in_=rr[o_i][:, 0:512]), [rk], ["stO"])
                p.op("dve", lambda e, o_i=o_i: e.bn_stats(out=statO[:, 6:12], in_=rr[o_i][:, 512:1024]), [rk], ["stO"])
                p.op("dve", lambda e: e.bn_aggr(out=statO[:, 12:14], in_=statO[:, 0:12]), ["stO"], ["mvO"])
                rstd_chain(statO, 1, "O")
                TS("dve", ro[o_i], rr[o_i], statO[:, 12:13], statO[:, 20:21], ALU.subtract, ALU.mult, [rk, "mvO", "rsO"], [ok])
                TT("pool", ro[o_i][:, 0:512], ro[o_i][:, 0:512], lng_rep[:, 0:512], ALU.mult, [ok, "lng"], [ok + "a"])
                TT("dve", ro[o_i][:, 512:1024], ro[o_i][:, 512:1024], lng_rep[:, 512:1024], ALU.mult, [ok, "lng"], [ok + "b"])
                TT("pool", ro[o_i][:, 0:512], ro[o_i][:, 0:512], lnb_rep[:, 0:512], ALU.add, [ok + "a", "lnb"], [ok + "a"])
                TT("dve", ro[o_i][:, 512:1024], ro[o_i][:, 512:1024], lnb_rep[:, 512:1024], ALU.add, [ok + "b", "lnb"], [ok + "b"])
                p.dma("sp", out_d[b, tok0 + s * 128:tok0 + (s + 1) * 128, :], ro[o_i], reads=[ok, ok + "a", ok + "b"],
                      writes=["out%d" % o_i], stream="o%d" % o_i, total=False)

        F0(0)
        for i in range(NT):
            if i + 1 < NT:
                F0(i + 1)
            MAIN(i)
        p.wait_all("sp", ["out0", "out1"])
        p.wait_all("act", ["out0", "out1"])
        p.emit(st)
        print("ops:", {e: len(p.ops[e]) for e in ENGS}, "arena top", A.top, "est_us", p.est_ns / 1e3)
    return nc


_CACHE = {}


def kernel(**inputs):
    f32 = lambda a: np.ascontiguousarray(np.asarray(a, dtype=np.float32))
    if "nc" not in _CACHE:
        _CACHE["nc"] = build_program()
    nc = _CACHE["nc"]
    x = f32(inputs["x"]); c = f32(inputs["c"]); ctx = f32(inputs["ctx"]); c_ctx = f32(inputs["c_ctx"])
    shared = dict(
        w_mod=f32(inputs["w_mod"][0]), b_mod=f32(inputs["b_mod"][0]), w_in=f32(inputs["w_in"][0]), b_in=f32(inputs["b_in"][0]),
        lam_re=f32(inputs["ssm_lam_re"][0]), lam_im=f32(inputs["ssm_lam_im"][0]), log_dt=f32(inputs["ssm_log_dt"][0]),
        b_re=f32(inputs["ssm_b_re"][0]), b_im=f32(inputs["ssm_b_im"][0]), c_re=f32(inputs["ssm_c_re"][0]), c_im=f32(inputs["ssm_c_im"][0]),
        ssm_d=f32(inputs["ssm_d"][0]), glu_w=f32(inputs["glu_w"][0]), glu_b=f32(inputs["glu_b"][0]),
        sgu_ln_g=f32(inputs["sgu_ln_g"][0]), sgu_ln_b=f32(inputs["sgu_ln_b"][0]), sgu_w=f32(inputs["sgu_w"][0]), sgu_b=f32(inputs["sgu_b"][0]),
        w_a=f32(inputs["w_branch_a"][0]), w_b=f32(inputs["w_branch_b"][0]), w_out=f32(inputs["w_out"][0]), b_out=f32(inputs["b_out"][0]),
        ln_g=f32(inputs["ln_g"][0]), ln_b=f32(inputs["ln_b"][0]),
    )
    shared.update(host_consts())
    in_maps = []
    for i in range(NCORES):
        sl = slice(i * BPC, (i + 1) * BPC)
        m = dict(shared)
        m["x"] = np.ascontiguousarray(x[sl])
        m["c_all"] = np.ascontiguousarray(np.concatenate([c[sl], c_ctx[None, :]], axis=0))
        m["ctx"] = np.ascontiguousarray(ctx[sl].reshape(BPC * CTX, D))
        in_maps.append(m)
    res = run_bass_kernel_spmd(nc, in_maps, core_ids=list(range(NCORES)))
    out = np.concatenate([np.asarray(r["out"]) for r in res.results], axis=0)
    return out.astype(np.float32)
```
